# Optimizing a Trainium2 kernel written in Bass

```python
import math
import jax, jax.numpy as jnp
from jax import lax
import numpy as np

D_MODEL = 2048
BATCH = 4
SEQ = 4096
DEPTH = 2

RWKV_HEAD = 64
RWKV_DIM = D_MODEL
RWKV_HEADS = RWKV_DIM // RWKV_HEAD
DECAY_LORA = max(32, int(round(1.8 * RWKV_DIM ** 0.5 / 32)) * 32)
ICLR_LORA = max(32, int(round(1.8 * RWKV_DIM ** 0.5 / 32)) * 32)
VRES_LORA = max(32, int(round(1.3 * RWKV_DIM ** 0.5 / 32)) * 32)
GATE_LORA = max(32, int(round(0.6 * RWKV_DIM ** 0.8 / 32)) * 32)
GN_EPS = 1e-5 * RWKV_HEAD
DIFF_HEAD = 128
DIFF_HEADS = D_MODEL // (2 * DIFF_HEAD)
QK_DIM = DIFF_HEADS * 2 * DIFF_HEAD
V_DIM = DIFF_HEADS * 2 * DIFF_HEAD
Q_BLOCK = 128
ROPE_THETA = 10000.0
SUBLN_EPS = 1e-5
FFN_HIDDEN = int(math.ceil(8 * D_MODEL / 3 / 256)) * 256
RWKV_COLS = 3 * RWKV_DIM + 2 * DECAY_LORA + 2 * ICLR_LORA + GATE_LORA
W_IN_COLS = RWKV_COLS + 2 * QK_DIM + V_DIM + 2 * D_MODEL
DEEPNORM_ALPHA = (2 * DEPTH) ** 0.25
DEEPNORM_BETA = (8 * DEPTH) ** -0.25
LN_EPS = 1e-5

kernel_name = "hybrid_rwkv7_diffattn_deepnorm_adaln_encoder"


def _offsets(sizes):
    out, acc = [], 0
    for s in sizes[:-1]:
        acc += s
        out.append(acc)
    return out


def layer_norm(x, g, b):
    xf = x.astype(jnp.float32)
    mu = jnp.mean(xf, -1, keepdims=True)
    var = jnp.mean(jnp.square(xf - mu), -1, keepdims=True)
    return ((xf - mu) * lax.rsqrt(var + LN_EPS)).astype(x.dtype) * g + b


def centred_token_shift(z, mu_prev, mu_next):
    z_prev = jnp.pad(z[:, :-1], ((0, 0), (1, 0), (0, 0)))
    z_next = jnp.pad(z[:, 1:], ((0, 0), (0, 1), (0, 0)))
    return z + mu_prev * (z_prev - z) + mu_next * (z_next - z)


def rope_tables(seq, dim, dtype):
    pos = jnp.arange(seq, dtype=jnp.float32)
    inv = ROPE_THETA ** (-jnp.arange(0, dim, 2, dtype=jnp.float32) / dim)
    ang = pos[:, None] * inv[None, :]
    emb = jnp.concatenate([ang, ang], axis=-1)
    return jnp.cos(emb).astype(dtype), jnp.sin(emb).astype(dtype)


def apply_rope(t, cos, sin):
    t1, t2 = jnp.split(t, 2, axis=-1)
    rot = jnp.concatenate([-t2, t1], axis=-1)
    c = cos[:, None, None, :]
    s = sin[:, None, None, :]
    return t * c + rot * s


def wkv7_scan(r, decay, k, v, kk, a, reverse):
    B, S, H, N = r.shape

    def step(state, inp):
        r_t, w_t, k_t, v_t, kk_t, a_t = inp
        s_kk = jnp.einsum('bhvk,bhk->bhv', state, kk_t)
        state = (state * w_t[:, :, None, :]
                 - s_kk[..., None] * (kk_t * a_t)[:, :, None, :]
                 + v_t[..., None] * k_t[:, :, None, :])
        return state, jnp.einsum('bhvk,bhk->bhv', state, r_t)

    xs = tuple(jnp.swapaxes(t, 0, 1) for t in (r, decay, k, v, kk, a))
    state0 = jnp.zeros((B, H, N, N), jnp.float32)
    _, out = lax.scan(step, state0, xs, reverse=reverse)
    return jnp.swapaxes(out, 0, 1)


def rwkv7_mixer(z, v_first, v_mix, decay_w0, decay_up, iclr_a0, iclr_up, gate_up,
                k_k, k_a, r_k, ln_w, ln_b):
    B, S, _ = z.shape
    r, k, v, xw_f, xw_b, xa_f, xa_b, xg = jnp.split(
        z, _offsets([RWKV_DIM] * 3 + [DECAY_LORA] * 2 + [ICLR_LORA] * 2 + [GATE_LORA]), axis=-1)
    if v_mix is None:
        v_first = v
    else:
        v = v + (v_first - v) * v_mix

    def heads(t):
        return t.reshape(B, S, RWKV_HEADS, RWKV_HEAD).astype(jnp.float32)

    r_h, v_h = heads(r), heads(v)
    kk_h = heads(k * k_k)
    kk_h = kk_h * lax.rsqrt(jnp.maximum(jnp.sum(kk_h * kk_h, -1, keepdims=True), 1e-24))

    o_sum = None
    k_sum = None
    for d, (xw, xa) in enumerate(((xw_f, xa_f), (xw_b, xa_b))):
        w_log = -jax.nn.softplus(-(decay_w0[d] + jnp.tanh(xw) @ decay_up[d])) - 0.5
        decay = jnp.exp(-jnp.exp(heads(w_log)))
        a = jax.nn.sigmoid(iclr_a0[d] + xa @ iclr_up[d])
        k_d = heads(k * (1 + (a - 1) * k_a))
        o_d = wkv7_scan(r_h, decay, k_d, v_h, kk_h, heads(a), reverse=(d == 1))
        o_sum = o_d if o_sum is None else o_sum + o_d
        k_sum = k_d if k_sum is None else k_sum + k_d

    mu = jnp.mean(o_sum, -1, keepdims=True)
    var = jnp.mean(jnp.square(o_sum - mu), -1, keepdims=True)
    o_n = ((o_sum - mu) * lax.rsqrt(var + GN_EPS)).reshape(B, S, RWKV_DIM)
    o_n = o_n * ln_w.astype(jnp.float32) + ln_b.astype(jnp.float32)
    bonus = (jnp.sum(r_h * k_sum * r_k.astype(jnp.float32), -1, keepdims=True) * v_h).reshape(B, S, RWKV_DIM)
    g = jax.nn.sigmoid(xg) @ gate_up
    y = (o_n + bonus).astype(z.dtype) * g
    return y, v_first


def diff_attention_mixer(zq, zk, zv, cos, sin, lq1, lk1, lq2, lk2, subln_w, lam_init):
    B, S, _ = zq.shape
    q = apply_rope(zq.reshape(B, S, DIFF_HEADS, 2, DIFF_HEAD), cos, sin) * (DIFF_HEAD ** -0.5)
    k = apply_rope(zk.reshape(B, S, DIFF_HEADS, 2, DIFF_HEAD), cos, sin)
    v = zv.reshape(B, S, DIFF_HEADS, 2 * DIFF_HEAD)
    lam = (jnp.exp(jnp.sum(lq1.astype(jnp.float32) * lk1.astype(jnp.float32)))
           - jnp.exp(jnp.sum(lq2.astype(jnp.float32) * lk2.astype(jnp.float32))) + lam_init)
    qb = min(Q_BLOCK, S)
    nb = S // qb
    q_blocks = jnp.moveaxis(q.reshape(B, nb, qb, DIFF_HEADS, 2, DIFF_HEAD), 1, 0)

    def attend(q_blk):
        s = jnp.einsum('bqhmd,bkhmd->bhmqk', q_blk, k).astype(jnp.float32)
        p = jax.nn.softmax(s, axis=-1)
        w = (p[:, :, 0] - lam * p[:, :, 1]).astype(v.dtype)
        return jnp.einsum('bhqk,bkhe->bqhe', w, v)

    o = lax.map(attend, q_blocks)
    o = jnp.moveaxis(o, 0, 1).reshape(B, S, DIFF_HEADS, 2 * DIFF_HEAD).astype(jnp.float32)
    o = o * lax.rsqrt(jnp.mean(o * o, -1, keepdims=True) + SUBLN_EPS) * subln_w.astype(jnp.float32)
    return (o * (1.0 - lam_init)).astype(zq.dtype).reshape(B, S, V_DIM)


def setup_inputs(seed: int = 0) -> dict:
    key = jax.random.key(seed)
    ks = iter(jax.random.split(key, 40))
    f32 = jnp.float32

    def nrm(shape, scale):
        return jax.random.normal(next(ks), shape, f32) * scale

    def unif(shape, lo, hi):
        return jax.random.uniform(next(ks), shape, f32, lo, hi)

    L = DEPTH
    return {
        "x": nrm((BATCH, SEQ, D_MODEL), 1.0),
        "c": nrm((BATCH, D_MODEL), 1.0),
        "ada_w": nrm((L, D_MODEL, 6 * D_MODEL), 0.5 * D_MODEL ** -0.5),
        "ada_b": nrm((L, 6 * D_MODEL), 0.01),
        "w_in": nrm((L, D_MODEL, W_IN_COLS), D_MODEL ** -0.5),
        "shift_mu_prev": unif((L, RWKV_COLS), 0.0, 0.5),
        "shift_mu_next": unif((L, RWKV_COLS), 0.0, 0.5),
        "decay_w0": unif((L, 2, RWKV_DIM), -2.0, 1.0),
        "decay_up": nrm((L, 2, DECAY_LORA, RWKV_DIM), 0.5 * DECAY_LORA ** -0.5),
        "iclr_a0": unif((L, 2, RWKV_DIM), -1.0, 1.0),
        "iclr_up": nrm((L, 2, ICLR_LORA, RWKV_DIM), 0.5 * ICLR_LORA ** -0.5),
        "gate_up": nrm((L, GATE_LORA, RWKV_DIM), GATE_LORA ** -0.5),
        "k_k": 0.85 + nrm((L, RWKV_DIM), 0.05),
        "k_a": 1.0 + nrm((L, RWKV_DIM), 0.05),
        "r_k": nrm((L, RWKV_HEADS, RWKV_HEAD), 0.1),
        "ln_x_w": 1.0 + nrm((L, RWKV_DIM), 0.05),
        "ln_x_b": nrm((L, RWKV_DIM), 0.01),
        "vres_down": nrm((L - 1, D_MODEL, VRES_LORA), D_MODEL ** -0.5),
        "vres_up": nrm((L - 1, VRES_LORA, RWKV_DIM), 0.5 * VRES_LORA ** -0.5),
        "vres_v0": 1.0 + nrm((L - 1, RWKV_DIM), 0.1),
        "lambda_q1": nrm((L, DIFF_HEAD), 0.1),
        "lambda_k1": nrm((L, DIFF_HEAD), 0.1),
        "lambda_q2": nrm((L, DIFF_HEAD), 0.1),
        "lambda_k2": nrm((L, DIFF_HEAD), 0.1),
        "subln_w": 1.0 + nrm((L, 2 * DIFF_HEAD), 0.05),
        "proj_a": nrm((L, RWKV_DIM, D_MODEL), RWKV_DIM ** -0.5),
        "proj_b": nrm((L, V_DIM, D_MODEL), V_DIM ** -0.5),
        "w_out": nrm((L, D_MODEL, D_MODEL), DEEPNORM_BETA * D_MODEL ** -0.5),
        "ln1_g": 1.0 + nrm((L, D_MODEL), 0.05),
        "ln1_b": nrm((L, D_MODEL), 0.01),
        "ffn_w_gate": nrm((L, D_MODEL, FFN_HIDDEN), D_MODEL ** -0.5),
        "ffn_w_up": nrm((L, D_MODEL, FFN_HIDDEN), D_MODEL ** -0.5),
        "ffn_w_down": nrm((L, FFN_HIDDEN, D_MODEL), DEEPNORM_BETA * FFN_HIDDEN ** -0.5),
        "ln2_g": 1.0 + nrm((L, D_MODEL), 0.05),
        "ln2_b": nrm((L, D_MODEL), 0.01),
    }


def reference(x, c, ada_w, ada_b, w_in, shift_mu_prev, shift_mu_next, decay_w0, decay_up,
              iclr_a0, iclr_up, gate_up, k_k, k_a, r_k, ln_x_w, ln_x_b, vres_down, vres_up,
              vres_v0, lambda_q1, lambda_k1, lambda_q2, lambda_k2, subln_w, proj_a, proj_b,
              w_out, ln1_g, ln1_b, ffn_w_gate, ffn_w_up, ffn_w_down, ln2_g, ln2_b):
    B, S, _ = x.shape
    cos, sin = rope_tables(S, DIFF_HEAD, x.dtype)
    c_act = jax.nn.silu(c)
    in_cuts = _offsets([RWKV_COLS, QK_DIM, QK_DIM, V_DIM, D_MODEL, D_MODEL])
    v_first = None
    for l in range(DEPTH):
        mod = (c_act @ ada_w[l] + ada_b[l])[:, None, :]
        sh_m, sc_m, g_m, sh_f, sc_f, g_f = jnp.split(mod, 6, axis=-1)

        u = x * (1 + sc_m) + sh_m
        z = u @ w_in[l]
        z_rwkv, z_q, z_k, z_v, z_ga, z_gb = jnp.split(z, in_cuts, axis=-1)
        z_rwkv = centred_token_shift(z_rwkv, shift_mu_prev[l], shift_mu_next[l])
        v_mix = None if l == 0 else jax.nn.sigmoid(vres_v0[l - 1] + (u @ vres_down[l - 1]) @ vres_up[l - 1])
        y_a, v_first = rwkv7_mixer(z_rwkv, v_first, v_mix, decay_w0[l], decay_up[l], iclr_a0[l],
                                   iclr_up[l], gate_up[l], k_k[l], k_a[l], r_k[l], ln_x_w[l], ln_x_b[l])
        lam_init = 0.8 - 0.6 * math.exp(-0.3 * l)
        y_b = diff_attention_mixer(z_q, z_k, z_v, cos, sin, lambda_q1[l], lambda_k1[l],
                                   lambda_q2[l], lambda_k2[l], subln_w[l], lam_init)
        merged = jax.nn.sigmoid(z_ga) * (y_a @ proj_a[l]) + jax.nn.sigmoid(z_gb) * (y_b @ proj_b[l])
        x = layer_norm(DEEPNORM_ALPHA * x + g_m * (merged @ w_out[l]), ln1_g[l], ln1_b[l])

        u = x * (1 + sc_f) + sh_f
        h = jax.nn.silu(u @ ffn_w_gate[l]) * (u @ ffn_w_up[l])
        x = layer_norm(DEEPNORM_ALPHA * x + g_f * (h @ ffn_w_down[l]), ln2_g[l], ln2_b[l])
    return x
```

```python
import numpy as np
import concourse.bass as bass
import concourse.mybir as mybir

F32 = mybir.dt.float32
BF16 = mybir.dt.bfloat16
AF = mybir.ActivationFunctionType
ALU = mybir.AluOpType
AX = mybir.AxisListType

ENGS = ("tensor", "vector", "scalar", "gpsimd", "sync")


class Buf:
    __slots__ = ("name", "lw", "rd", "chan")

    def __init__(self, name, chan=None):
        self.name = name
        self.lw = None
        self.rd = {}
        self.chan = chan


class Chan:
    __slots__ = ("key", "count", "unit")

    def __init__(self, key, unit=16):
        self.key = key
        self.count = 0
        self.unit = unit


class Prog:
    def __init__(self, nc, stack, nchan=64):
        self.nc = nc
        self.stack = stack
        self.sems = {}
        self.ops = {e: [] for e in ENGS}
        self.ecnt = {e: 0 for e in ENGS}
        self.seen = {e: {} for e in ENGS}
        for e in ENGS:
            self.sems["e_" + e] = stack.enter_context(nc.semaphore("se_" + e))
        self.chans = []
        for i in range(nchan):
            k = "c%d" % i
            self.sems[k] = stack.enter_context(nc.semaphore("sc%d" % i))
            self.chans.append(Chan(k))
        self._nextchan = 0
        self.chan_by_key = {c.key: c for c in self.chans}
        self.cchans = []
        self.cfree = []
        for i in range(1):
            kk = "cc%d" % i
            self.sems[kk] = stack.enter_context(nc.semaphore("s" + kk))
            c = Chan(kk, unit=1)
            self.chan_by_key[kk] = c
            self.cfree.append(c)
        self.nbuf = 0

    def chan(self):
        c = self.chans[self._nextchan % len(self.chans)]
        self._nextchan += 1
        return c

    def buf(self, name=None, chan=None):
        self.nbuf += 1
        return Buf(name or ("b%d" % self.nbuf), chan)

    def _need(self, eng, ev, waits):
        if ev is None:
            return
        k, v = ev
        if k in self.chan_by_key:
            v = self.chan_by_key[k].unit * self.chan_by_key[k].count
        if self.seen[eng].get(k, 0) >= v:
            return
        self.seen[eng][k] = v
        waits[k] = max(waits.get(k, 0), v)

    def coll_chan(self):
        c = self.cfree.pop(0)
        self.cchans.append(c)
        return c

    def op(self, eng, fn, reads=(), writes=(), acc=False, dma=False, coll=False):
        waits = {}
        own = "e_" + eng
        for b in reads:
            if b.lw is not None:
                self._need(eng, b.lw, waits)
        for b in writes:
            if b.lw is not None:
                k = b.lw[0]
                if not (k == own and not dma):
                    self._need(eng, b.lw, waits)
            for k, v in b.rd.items():
                if k == own and not dma:
                    continue
                self._need(eng, (k, v), waits)
        if coll:
            ch = self.cfree[0]
            if ch not in self.cchans:
                self.cchans.append(ch)
            ch.count += 1
            ev = (ch.key, ch.count)
            inc = (ch.key, None)
        elif dma:
            ch = None
            for b in writes:
                if b.chan is None:
                    b.chan = self.chan()
                ch = b.chan
            assert ch is not None
            ch.count += 1
            ev = (ch.key, 16 * ch.count)
            inc = (ch.key, 16)
        else:
            self.ecnt[eng] += 1
            ev = (own, self.ecnt[eng])
            inc = (own, 1)
            self.seen[eng][own] = max(self.seen[eng].get(own, 0), 0)
        for b in reads:
            b.rd[ev[0]] = max(b.rd.get(ev[0], 0), ev[1])
        for b in writes:
            b.lw = ev
            b.rd = {}
        self.ops[eng].append((tuple(waits.items()), fn, inc))

    def finish(self):
        waits = []
        for c in self.chans + self.cchans:
            if c.count:
                waits.append((c.key, c.unit * c.count))
        for e in ENGS:
            if e != "sync" and self.ecnt[e]:
                waits.append(("e_" + e, self.ecnt[e]))
        self.ops["sync"].append((tuple(waits), None, None))

    def emit(self):
        nc = self.nc
        sems = self.sems

        def run(e, name):
            for waits, fn, inc in self.ops[name]:
                for k, v in waits:
                    e.wait_ge(sems[k], v)
                if fn is not None:
                    ins = fn(e)
                    if inc[1] is None:
                        ins.then_inc(sems[inc[0]])
                    else:
                        ins.then_inc(sems[inc[0]], inc[1])

        with nc.Block() as block:
            @block.tensor
            def _(e):
                run(e, "tensor")

            @block.vector
            def _(e):
                run(e, "vector")

            @block.scalar
            def _(e):
                run(e, "scalar")

            @block.gpsimd
            def _(e):
                run(e, "gpsimd")

            @block.sync
            def _(e):
                run(e, "sync")


from contextlib import ExitStack
from concourse.bass_utils import run_bass_kernel_spmd

D = 2048
T = 4096
NT = 2048
KC = 16
FH = 5632
HC = 44
ALPHA = 4 ** 0.25
LN_EPS = 1e-5


class TV:
    __slots__ = ("ap", "buf")

    def __init__(self, ap, buf):
        self.ap = ap
        self.buf = buf


class TL:
    def __init__(self, base, buf):
        self.base = base
        self.buf = buf

    def __getitem__(self, idx):
        return TV(self.base[idx], self.buf)


class K:
    def __init__(self, nc, stack, nchan=80):
        self.nc = nc
        self.st = stack
        self.p = Prog(nc, stack, nchan=nchan)
        self.n = 0
        self.rr = 0

    def name(self, s):
        self.n += 1
        return "%s_%d" % (s, self.n)

    def sb(self, shape, dt=F32, stack=None, name="t"):
        h = (stack or self.st).enter_context(self.nc.sbuf_tensor(self.name(name), list(shape), dt))
        return TL(h, self.p.buf())

    def ps(self, shape=(128, 512), dt=F32, stack=None):
        h = (stack or self.st).enter_context(self.nc.psum_tensor(self.name("ps"), list(shape), dt))
        return TL(h, self.p.buf())

    def dram(self, name, shape, dt=F32, kind="Internal"):
        h = self.nc.dram_tensor(name, list(shape), dt, kind=kind)
        return TL(h.ap(), self.p.buf())

    def mm(self, out, lhsT, rhs, start=True, stop=True):
        self.p.op("tensor", lambda e: e.matmul(out.ap, lhsT=lhsT.ap, rhs=rhs.ap, start=start, stop=stop),
                  reads=[lhsT.buf, rhs.buf], writes=[out.buf])

    def tr(self, out, in_, ident):
        self.p.op("tensor", lambda e: e.transpose(out.ap, in_.ap, ident.ap),
                  reads=[in_.buf, ident.buf], writes=[out.buf])

    def act(self, out, in_, func, bias=None, scale=None, accum=None):
        rd = [in_.buf]
        kw = {}
        if bias is not None:
            if isinstance(bias, TV):
                rd.append(bias.buf)
                kw["bias"] = bias.ap
            else:
                kw["bias"] = bias
        if scale is not None:
            if isinstance(scale, TV):
                rd.append(scale.buf)
                kw["scale"] = scale.ap
            else:
                kw["scale"] = scale
        wr = [out.buf]
        if accum is not None:
            kw["accum_out"] = accum.ap
            wr.append(accum.buf)
        self.p.op("scalar", lambda e: e.activation(out=out.ap, in_=in_.ap, func=func, **kw), reads=rd, writes=wr)

    def tt(self, eng, out, a, b, op):
        self.p.op(eng, lambda e: e.tensor_tensor(out=out.ap, in0=a.ap, in1=b.ap, op=op),
                  reads=[a.buf, b.buf], writes=[out.buf])

    def ts(self, eng, out, a, s1, s2, op0, op1=None):
        rd = [a.buf]
        v1 = s1.ap if isinstance(s1, TV) else s1
        v2 = s2.ap if isinstance(s2, TV) else s2
        if isinstance(s1, TV):
            rd.append(s1.buf)
        if isinstance(s2, TV):
            rd.append(s2.buf)
        if op1 is None:
            self.p.op(eng, lambda e: e.tensor_scalar(out=out.ap, in0=a.ap, scalar1=v1, scalar2=None, op0=op0),
                      reads=rd, writes=[out.buf])
        else:
            self.p.op(eng, lambda e: e.tensor_scalar(out=out.ap, in0=a.ap, scalar1=v1, scalar2=v2, op0=op0, op1=op1),
                      reads=rd, writes=[out.buf])

    def stt(self, out, a, s, b, op0, op1):
        rd = [a.buf, b.buf]
        v = s.ap if isinstance(s, TV) else s
        if isinstance(s, TV):
            rd.append(s.buf)
        self.p.op("vector", lambda e: e.scalar_tensor_tensor(out=out.ap, in0=a.ap, scalar=v, in1=b.ap, op0=op0, op1=op1),
                  reads=rd, writes=[out.buf])

    def copy(self, eng, out, in_):
        if eng == "scalar":
            self.p.op(eng, lambda e: e.copy(out=out.ap, in_=in_.ap), reads=[in_.buf], writes=[out.buf])
        else:
            self.p.op(eng, lambda e: e.tensor_copy(out=out.ap, in_=in_.ap), reads=[in_.buf], writes=[out.buf])

    def memset(self, eng, out, val):
        self.p.op(eng, lambda e: e.memset(out.ap, val), writes=[out.buf])

    def recip(self, out, in_):
        self.p.op("vector", lambda e: e.reciprocal(out=out.ap, in_=in_.ap), reads=[in_.buf], writes=[out.buf])

    def scan(self, out, d0, d1, init, op0, op1):
        self.p.op("vector", lambda e: e.tensor_tensor_scan(out=out.ap, data0=d0.ap, data1=d1.ap, initial=init, op0=op0, op1=op1),
                  reads=[d0.buf, d1.buf], writes=[out.buf])

    def dma(self, out, in_, eng=None, slow=False):
        if eng is None or eng == "gpsimd":
            eng = "sync"
        if slow:
            self.p.op(eng, lambda e: e.dma_start(out=out.ap, in_=in_.ap, allow_slow_non_contiguous=True),
                      reads=[in_.buf], writes=[out.buf], dma=True)
        else:
            self.p.op(eng, lambda e: e.dma_start(out=out.ap, in_=in_.ap), reads=[in_.buf], writes=[out.buf], dma=True)

    def allgather(self, out, in_):
        groups = [[0, 1], [2, 3], [4, 5], [6, 7]]
        self.p.op("gpsimd", lambda e: e.collective_compute("AllGather", ALU.bypass, replica_groups=groups, ins=[in_.ap], outs=[out.ap]),
                  reads=[in_.buf], writes=[out.buf], coll=True)

    def allgather_pieces(self, G5, src, rows):
        n = src.base.shape[0] // rows
        for j in range(n):
            self.allgather(TV(G5.base[j], G5.buf), TV(src.base[j * rows:(j + 1) * rows, :], src.buf))

    def barrier(self):
        p = self.p
        for e in ENGS:
            waits = {}
            for c in p.chans + p.cchans:
                if c.count:
                    p._need(e, (c.key, c.unit * c.count), waits)
            for e2 in ENGS:
                if e2 != e and p.ecnt[e2]:
                    p._need(e, ("e_" + e2, p.ecnt[e2]), waits)
            if waits:
                p.ops[e].append((tuple(waits.items()), None, None))


def emit_mod(k, cT, ada_w, ada_bT, mod_out, nl=2):
    st = ExitStack()
    cs = k.sb([128, KC], stack=st)
    sc = k.sb([128, KC], stack=st)
    k.dma(cs[:, :], cT[:, :], eng="sync")
    k.act(sc[:, :], cs[:, :], AF.Silu)
    wst = [k.sb([128, KC, 512], stack=st, name="adaw") for _ in range(2)]
    ps = k.ps(stack=st)
    bt = k.sb([128, nl, 96], stack=st)
    res = k.sb([128, nl, 96], stack=st)
    for l in range(nl):
        k.dma(bt[:, l, :], ada_bT[l, :, :], eng="sync")
    g = 0
    for l in range(nl):
        for mg in range(24):
            w = wst[g % 2]
            g += 1
            for kh in range(4):
                k.dma(w[:, kh * 4:(kh + 1) * 4, :],
                      TV(ada_w.base[l, kh * 512:(kh + 1) * 512, mg * 512:(mg + 1) * 512].rearrange("(c p) n -> p c n", p=128), ada_w.buf))
            for mi in range(4):
                m = mg * 4 + mi
                for kc in range(KC):
                    k.mm(ps[:, l * 96 + m:l * 96 + m + 1], w[:, kc, mi * 128:(mi + 1) * 128], sc[:, kc:kc + 1],
                         start=(kc == 0), stop=(kc == KC - 1))
        k.tt("vector", res[:, l, :], ps[:, l * 96:(l + 1) * 96], bt[:, l, :], ALU.add)
        k.dma(mod_out[l, :, :], res[:, l, :], eng="sync")
    k.barrier()
    st.close()


class WStream:
    def __init__(self, k, stack, kmax=22, nbuf=3):
        self.k = k
        self.stg = [k.sb([128, kmax, 128], stack=stack, name="wstg") for _ in range(2)]
        self.wb = [k.sb([128, kmax, 128], BF16, stack=stack, name="wbf") for _ in range(nbuf)]
        self.i = 0
        self.nbuf = nbuf

    def unit(self, w, r0, nk, c0, ncol=128):
        k = self.k
        s = self.stg[self.i % 2]
        b = self.wb[self.i % self.nbuf]
        ce = ("scalar", "vector")[self.i % 2]
        self.i += 1
        src = TV(w.base[r0 * 128:(r0 + nk) * 128, c0:c0 + ncol].rearrange("(c p) n -> p c n", p=128), w.buf)
        k.dma(s[:, 0:nk, 0:ncol], src, eng="sync")
        k.copy(ce, b[:, 0:nk, 0:ncol], s[:, 0:nk, 0:ncol])
        return b


def layer_norm_fm(k, S1, nch, ntok, onesD, psM, psV, mean_sb, sqt, rstd, g, b, after=None):
    for m in range(nch):
        k.mm(psM[:, 0:ntok], onesD[:, :], S1[:, m, :], start=(m == 0), stop=(m == nch - 1))
    k.copy("scalar", mean_sb[:, 0:ntok], psM[:, 0:ntok])
    for m in range(nch):
        k.tt(("vector", "gpsimd")[m % 2], S1[:, m, :], S1[:, m, :], mean_sb[:, 0:ntok], ALU.subtract)
    for m in range(nch):
        q = sqt[m % 2]
        k.act(q[:, 0:ntok], S1[:, m, :], AF.Square)
        k.mm(psV[:, 0:ntok], onesD[:, :], q[:, 0:ntok], start=(m == 0), stop=(m == nch - 1))
    k.act(rstd[:, 0:ntok], psV[:, 0:ntok], AF.Sqrt, bias=k.eps_ln[:, :])
    k.recip(rstd[:, 0:ntok], rstd[:, 0:ntok])
    for m in range(nch):
        k.tt("vector", S1[:, m, :], S1[:, m, :], rstd[:, 0:ntok], ALU.mult)
        k.act(S1[:, m, :], S1[:, m, :], AF.Identity, bias=b[:, m:m + 1], scale=g[:, m:m + 1])
        if after is not None:
            after(m)


def emit_phase3(k, l, xT, GA, GB, sel, wga, wgb, modT, lnv, proj_a, proj_b, w_out, wg, wu, wd, x_out, consts):
    st = ExitStack()
    TT = 512
    mod = k.sb([128, 96], stack=st)
    op1 = k.sb([128, 96], stack=st)
    lv = k.sb([128, 4, KC], stack=st)
    onesD = k.sb([128, 128], stack=st)
    k.eps_ln = k.sb([128, 1], stack=st)
    k.dma(mod[:, :], modT[:, :], eng="sync")
    k.dma(lv[:, :, :], lnv[:, :, :], eng="sync")
    selt = k.sb([128, 2], stack=st)
    k.dma(selt[:, :], sel[:, :], eng="sync")
    k.ts("vector", op1[:, :], mod[:, :], 1.0, None, ALU.add)
    k.memset("gpsimd", onesD[:, :], 1.0 / D)
    k.memset("gpsimd", k.eps_ln[:, :], LN_EPS)
    S1 = k.sb([128, KC, TT], stack=st, name="S1")
    Y1 = k.sb([128, KC, TT], BF16, stack=st, name="Y1")
    Y2 = k.sb([128, KC, TT], BF16, stack=st, name="Y2")
    MG = k.sb([128, KC, TT], BF16, stack=st, name="MG")
    Y3 = k.sb([128, KC, TT], BF16, stack=st, name="Y3")
    HT = k.sb([128, HC, TT], BF16, stack=st, name="HT")
    tmp = [k.sb([128, TT], stack=st, name="tmp") for _ in range(4)]
    gin = [k.sb([128, TT], stack=st, name="gin") for _ in range(4)]
    mean_sb = k.sb([128, TT], stack=st)
    rstd = k.sb([128, TT], stack=st)
    ws = WStream(k, st, nbuf=3)
    banks = [k.ps(stack=st) for _ in range(6)]
    psM = k.ps(stack=st)
    psV = k.ps(stack=st)
    bi = [0]

    def bank():
        b = banks[bi[0] % 6]
        bi[0] += 1
        return b

    G_M, SH_F, SC_F, G_F = 2 * KC, 3 * KC, 4 * KC, 5 * KC
    for t in range(NT // TT):
        tsl = slice(t * TT, (t + 1) * TT)
        for half in range(2):
            k.dma(S1[:, half * 8:(half + 1) * 8, :],
                  TV(xT.base[half * 8:(half + 1) * 8, :, tsl].rearrange("c p t -> p c t"), xT.buf))
        for kc in range(KC):
            k.ts(("vector", "gpsimd")[kc % 2], Y3[:, kc, :], S1[:, kc, :], op1[:, KC + kc:KC + kc + 1], mod[:, kc:kc + 1], ALU.mult, ALU.add)
        for src, dst in ((GA, Y1), (GB, Y2)):
            for fh in range(2):
                for th in range(2):
                    k.dma(S1[:, th * 8:(th + 1) * 8, :],
                          TV(src.base[fh, :, :, th * NT + t * TT:th * NT + (t + 1) * TT].rearrange("c p t -> p c t"), src.buf))
                k.ts("vector", S1[:, 0:8, :], S1[:, 0:8, :], selt[:, 0:1], None, ALU.mult)
                k.stt(dst[:, fh * 8:(fh + 1) * 8, :], S1[:, 8:16, :], selt[:, 1:2], S1[:, 0:8, :], ALU.mult, ALU.add)
        for m in range(KC):
            ga, gb = gin[(2 * m) % 4], gin[(2 * m + 1) % 4]
            wga_u = ws.unit(wga, 0, KC, m * 128)
            wgb_u = ws.unit(wgb, 0, KC, m * 128)
            pga, pgb = bank(), bank()
            for kc in range(KC):
                k.mm(pga[:, :], wga_u[:, kc, :], Y3[:, kc, :], start=(kc == 0), stop=(kc == KC - 1))
            for kc in range(KC):
                k.mm(pgb[:, :], wgb_u[:, kc, :], Y3[:, kc, :], start=(kc == 0), stop=(kc == KC - 1))
            k.act(ga[:, :], pga[:, :], AF.Sigmoid)
            k.act(gb[:, :], pgb[:, :], AF.Sigmoid)
            wa = ws.unit(proj_a, 0, KC, m * 128)
            wb_ = ws.unit(proj_b, 0, KC, m * 128)
            pa, pb = bank(), bank()
            for kc in range(KC):
                k.mm(pa[:, :], wa[:, kc, :], Y1[:, kc, :], start=(kc == 0), stop=(kc == KC - 1))
            for kc in range(KC):
                k.mm(pb[:, :], wb_[:, kc, :], Y2[:, kc, :], start=(kc == 0), stop=(kc == KC - 1))
            t1, t2 = tmp[(2 * m) % 4], tmp[(2 * m + 1) % 4]
            k.tt("vector", t1[:, :], pa[:, :], ga[:, :], ALU.mult)
            k.tt("vector", t2[:, :], pb[:, :], gb[:, :], ALU.mult)
            k.tt("gpsimd", MG[:, m, :], t1[:, :], t2[:, :], ALU.add)
        for m in range(KC):
            wo = ws.unit(w_out, 0, KC, m * 128)
            pw = bank()
            xi = gin[m % 4]
            k.dma(xi[:, :], TV(xT.base[m, :, tsl], xT.buf))
            for kc in range(KC):
                k.mm(pw[:, :], wo[:, kc, :], MG[:, kc, :], start=(kc == 0), stop=(kc == KC - 1))
            xa = tmp[m % 4]
            k.act(xa[:, :], xi[:, :], AF.Copy, scale=float(ALPHA))
            k.stt(S1[:, m, :], pw[:, :], mod[:, G_M + m:G_M + m + 1], xa[:, :], ALU.mult, ALU.add)

        def mk_u2(m):
            k.ts("vector", Y1[:, m, :], S1[:, m, :], op1[:, SC_F + m:SC_F + m + 1], mod[:, SH_F + m:SH_F + m + 1], ALU.mult, ALU.add)

        layer_norm_fm(k, S1, KC, TT, onesD, psM, psV, mean_sb, tmp[0:2], rstd, TL(lv.base[:, 0, :], lv.buf), TL(lv.base[:, 1, :], lv.buf), after=mk_u2)
        for j in range(HC):
            wgu = ws.unit(wg, 0, KC, j * 128)
            wuu = ws.unit(wu, 0, KC, j * 128)
            pg, pu = bank(), bank()
            for kc in range(KC):
                k.mm(pg[:, :], wgu[:, kc, :], Y1[:, kc, :], start=(kc == 0), stop=(kc == KC - 1))
            for kc in range(KC):
                k.mm(pu[:, :], wuu[:, kc, :], Y1[:, kc, :], start=(kc == 0), stop=(kc == KC - 1))
            sg = tmp[j % 4]
            k.act(sg[:, :], pg[:, :], AF.Silu)
            k.tt("vector", HT[:, j, :], sg[:, :], pu[:, :], ALU.mult)
        for m in range(KC):
            w1 = ws.unit(wd, 0, 22, m * 128)
            w2 = ws.unit(wd, 22, 22, m * 128)
            pd = bank()
            for j in range(HC):
                w = w1 if j < 22 else w2
                k.mm(pd[:, :], w[:, j % 22, :], HT[:, j, :], start=(j == 0), stop=(j == HC - 1))
            xa = tmp[m % 4]
            k.act(xa[:, :], S1[:, m, :], AF.Copy, scale=float(ALPHA))
            k.stt(S1[:, m, :], pd[:, :], mod[:, G_F + m:G_F + m + 1], xa[:, :], ALU.mult, ALU.add)
        layer_norm_fm(k, S1, KC, TT, onesD, psM, psV, mean_sb, tmp[0:2], rstd, TL(lv.base[:, 2, :], lv.buf), TL(lv.base[:, 3, :], lv.buf))
        for half in range(2):
            k.dma(TV(x_out.base[half * 8:(half + 1) * 8, :, tsl].rearrange("c p t -> p c t"), x_out.buf),
                  S1[:, half * 8:(half + 1) * 8, :])
    k.barrier()
    st.close()


def _fm(v, nch):
    return np.ascontiguousarray(np.asarray(v, np.float32).reshape(nch, 128).T)


def build_B():
    nc = bass.Bass("TRN2", target_bir_lowering=False)
    with ExitStack() as st:
        k = K(nc, st)
        IN = lambda n, s: k.dram(n, s, kind="ExternalInput")
        xT, yaT, ybT = [IN(n, [KC, 128, NT]) for n in ("xT", "yaT", "ybT")]
        modT = IN("modT", [128, 96])
        lnv = IN("lnv", [128, 4, KC])
        proj_a, proj_b, w_out, wga, wgb = [IN(n, [D, D]) for n in ("proj_a", "proj_b", "w_out", "wga", "wgb")]
        wg, wu = IN("wg", [D, FH]), IN("wu", [D, FH])
        wd = IN("wd", [FH, D])
        x_out = k.dram("x_out", [KC, 128, NT], kind="ExternalOutput")
        emit_phase3(k, 0, xT, yaT, ybT, wga, wgb, modT, lnv, proj_a, proj_b, w_out, wg, wu, wd, x_out, None)
        print('B sbuf remaining', nc.sbuf_bytes_remaining)
        k.p.finish()
        k.p.emit()
    return nc


def build_M():
    nc = bass.Bass("TRN2", target_bir_lowering=False)
    with ExitStack() as st:
        k = K(nc, st)
        cT = k.dram("cT", [128, KC], kind="ExternalInput")
        ada_w = k.dram("ada_w", [2, D, 6 * D], kind="ExternalInput")
        ada_bT = k.dram("ada_bT", [2, 128, 96], kind="ExternalInput")
        mod_out = k.dram("mod", [2, 128, 96], kind="ExternalOutput")
        emit_mod(k, cT, ada_w, ada_bT, mod_out)
        k.p.finish()
        k.p.emit()
    return nc


import math
G = 4
CH = 64
TG = G * CH
SH_M, SC_M = 0, KC
C_ID, C_ROT, C_B1, C_B64, C_NSU, C_NSL, C_SU, C_SL, C_IU, C_IL = range(10)
NCM = 10
V_MP, V_MN, V_PF = 0, 30, 60
(P_KK, P_KA, P_RK, P_LNW, P_LNB, P_W0F, P_W0B, P_A0F, P_A0B, P_V0) = range(10)
NV = 60 + 80


def host_consts():
    cm = np.zeros((128, NCM, 128), np.float32)
    p = np.arange(128)
    cm[p, C_ID, p] = 1.0
    for m in range(128):
        if m < 64:
            cm[m + 64, C_ROT, m] = -1.0
        else:
            cm[m - 64, C_ROT, m] = 1.0
    blk = (p[:, None] // 64) == (p[None, :] // 64)
    cm[:, C_B1, :] = blk
    cm[:, C_B64, :] = blk / 64.0
    j = p[:, None] % 64
    t = p[None, :] % 64
    cm[:, C_NSU, :] = -1.0 * (blk & (j < t))
    cm[:, C_NSL, :] = -1.0 * (blk & (j > t))
    cm[:, C_SU, :] = (blk & (j < t))
    cm[:, C_SL, :] = (blk & (j > t))
    cm[:, C_IU, 0:64] = (j <= np.arange(64)[None, :])
    cm[:, C_IL, 0:64] = (j >= np.arange(64)[None, :])
    m01 = np.ones((128, TG), np.float32)
    m01[:, ::CH] = 0.0
    pos = np.arange(T, dtype=np.float32)
    inv = (np.float32(10000.0) ** (-np.arange(0, 128, 2, dtype=np.float32) / np.float32(128))).astype(np.float32)
    ang = pos[:, None] * inv[None, :]
    emb = np.concatenate([ang, ang], -1)
    cosT = np.ascontiguousarray(np.cos(emb).astype(np.float32).T)
    sinT = np.ascontiguousarray(np.sin(emb).astype(np.float32).T)
    return cm, m01, cosT, sinT


def emit_phaseA(k, l, xsrc, modT, Wc, vec, lwd, lwi, lwg, lwv, lamv, subw, cosT, sinT, cmat, m01d, vf_in, yaT, ybT, vf_out, Zs, ZsB):
    NRW = 54 + (1 if l == 1 else 0)
    lam_init = 0.8 - 0.6 * math.exp(-0.3 * l)
    GN_EPS = 1e-5 * 64

    def zrow(j, c0, c1):
        return TV(Zs.base[j, :, c0:c1], ZsB[j])

    st = ExitStack()
    mod = k.sb([128, 96], stack=st)
    op1 = k.sb([128, 96], stack=st)
    k.dma(mod[:, :], modT[:, :], eng="sync")
    k.ts("vector", op1[:, :], mod[:, :], 1.0, None, ALU.add)
    zt = k.sb([128, NRW, 1], stack=st)
    k.memset("gpsimd", zt[:, :, :], 0.0)
    for j in range(NRW):
        k.dma(zrow(j, 0, 1), zt[:, j, :], eng="sync", slow=True)
        k.dma(zrow(j, T + 1, T + 2), zt[:, j, :], eng="sync", slow=True)
    U = k.sb([128, KC, T], BF16, stack=st, name="U")
    zb = [k.sb([128, 2048], stack=st, name="zb") for _ in range(2)]
    for i in range(T // 128):
        xs = zb[i % 2]
        xv = TL(xs.base[:, :].rearrange("p (c t) -> p c t", c=KC), xs.buf)
        for dsl, srcv in xsrc(i):
            k.dma(xv[:, dsl, :], srcv)
        for kc in range(KC):
            k.ts(("vector", "gpsimd")[kc % 2], U[:, kc, i * 128:(i + 1) * 128], xv[:, kc, :],
                 op1[:, SC_M + kc:SC_M + kc + 1], mod[:, SH_M + kc:SH_M + kc + 1], ALU.mult, ALU.add)
    ws = WStream(k, st, kmax=KC, nbuf=3)
    banks = [k.ps(stack=st) for _ in range(8)]
    for j in range(NRW):
        w = ws.unit(Wc, 0, KC, j * 128)
        for th in range(2):
            for n in range(4):
                pb = banks[th * 4 + n]
                for kc in range(KC):
                    t0 = th * 2048 + n * 512
                    k.mm(pb[:, :], w[:, kc, :], U[:, kc, t0:t0 + 512], start=(kc == 0), stop=(kc == KC - 1))
            z = zb[th]
            for n in range(4):
                k.copy(("scalar", "vector")[n % 2], z[:, n * 512:(n + 1) * 512], banks[th * 4 + n][:, :])
            k.dma(zrow(j, 1 + th * 2048, 1 + (th + 1) * 2048), z[:, :])
    k.barrier()
    st.close()

    st = ExitStack()
    vecs = k.sb([128, NV], stack=st)
    k.dma(vecs[:, :], vec[:, :], eng="sync")
    c0 = k.sb([128, 30], stack=st)
    k.tt("vector", c0[:, :], vecs[:, V_MP:V_MP + 30], vecs[:, V_MN:V_MN + 30], ALU.add)
    k.ts("vector", c0[:, :], c0[:, :], -1.0, 1.0, ALU.mult, ALU.add)
    pf = lambda i, fc: vecs[:, V_PF + 8 * i + fc:V_PF + 8 * i + fc + 1]
    omka = k.sb([128, 8], stack=st)
    k.ts("vector", omka[:, :], vecs[:, V_PF + 8 * P_KA:V_PF + 8 * P_KA + 8], -1.0, 1.0, ALU.mult, ALU.add)
    k.omka2 = k.sb([128, 8], stack=st)
    k.ts("vector", k.omka2[:, :], vecs[:, V_PF + 8 * P_KA:V_PF + 8 * P_KA + 8], -2.0, 2.0, ALU.mult, ALU.add)
    k.eps_gn = k.sb([128, 1], stack=st)
    k.memset("gpsimd", k.eps_gn[:, :], GN_EPS)
    cm = k.sb([128, NCM, 128], stack=st)
    k.dma(cm[:, :, :], cmat[:, :, :], eng="sync")
    idb = k.sb([128, 128], BF16, stack=st)
    k.copy("vector", idb[:, :], cm[:, C_ID, :])
    m01 = k.sb([128, TG], stack=st)
    k.dma(m01[:, :], m01d[:, :], eng="sync")
    Rr, Kr, Vr, KKr = [k.sb([128, T], stack=st, name=n) for n in ("Rr", "Kr", "Vr", "KKr")]
    stg = Rr
    DEC = k.sb([128, 2, 8, 128], BF16, stack=st)
    ICL = k.sb([128, 2, 8, 128], BF16, stack=st)
    GAT = k.sb([128, 2, 8, 128], BF16, stack=st)
    VUP = k.sb([128, 8, 128], BF16, stack=st)
    for src, dst in ((lwd, DEC), (lwi, ICL), (lwg, GAT)):
        sv = TL(stg.base[:, 0:2048].rearrange("p (a b c) -> p a b c", a=2, b=8), stg.buf)
        k.dma(sv[:, :, :, :], src[:, :, :, :], eng="sync")
        k.copy("vector", dst[:, :, :, :], sv[:, :, :, :])
    if l == 1:
        sv = TL(stg.base[:, 0:1024].rearrange("p (b c) -> p b c", b=8), stg.buf)
        k.dma(sv[:, :, :], lwv[:, :, :], eng="sync")
        k.copy("vector", VUP[:, :, :], sv[:, :, :])
    LORd = TL(k.nc.dram_tensor("LORd%d" % l, [6, 128, T], BF16).ap(), k.p.buf())
    DNd = TL(k.nc.dram_tensor("DNd%d" % l, [128, T], BF16).ap(), k.p.buf()) if l == 1 else None
    SW = 512
    PP = [k.sb([128, 512], stack=st, name="pp%d" % i) for i in range(6)]
    lorp = k.sb([128, 4, 512], BF16, stack=st)
    zin = [k.sb([128, SW + 2], stack=st, name="zin")] * 2
    sht = PP[4:6]
    cnt = [0]

    def shift_tile(j, t0, n, out):
        i = cnt[0]
        cnt[0] += 1
        zi = zin[i % 2]
        t1 = sht[i % 2]
        k.dma(zi[:, 0:n + 2], zrow(j, t0, t0 + n + 2))
        k.act(t1[:, 0:n], zi[:, 1:n + 1], AF.Identity, scale=c0[:, j:j + 1])
        k.stt(t1[:, 0:n], zi[:, 0:n], vecs[:, V_MP + j:V_MP + j + 1], t1[:, 0:n], ALU.mult, ALU.add)
        k.stt(out, zi[:, 2:n + 2], vecs[:, V_MN + j:V_MN + j + 1], t1[:, 0:n], ALU.mult, ALU.add)

    lt = PP[0:2]
    lb = [TL(lorp.base[:, i, :], lorp.buf) for i in range(2)]
    for r in range(6):
        for t in range(T // SW):
            o = lt[(r * 8 + t) % 2]
            ob = lb[(r * 8 + t) % 2]
            shift_tile(24 + r, t * SW, SW, o[:, :])
            fn = (AF.Tanh, AF.Tanh, AF.Identity, AF.Identity, AF.Sigmoid, AF.Sigmoid)[r]
            k.act(ob[:, :], o[:, :], fn)
            k.dma(TV(LORd.base[r, :, t * SW:(t + 1) * SW], LORd.buf), ob[:, :])
    if l == 1:
        for t in range(T // SW):
            zi = zin[t % 2]
            k.dma(zi[:, 0:SW], zrow(54, 1 + t * SW, 1 + (t + 1) * SW))
            k.copy("vector", lb[t % 2][:, :], zi[:, 0:SW])
            k.dma(TV(DNd.base[:, t * SW:(t + 1) * SW], DNd.buf), lb[t % 2][:, :])

    import types
    import itertools
    OD = [TL(k.nc.dram_tensor("OD%d_%d" % (l, d), [128, T], F32).ap(), k.p.buf()) for d in range(2)]

    def mk_stream(si):
        R = types.SimpleNamespace()
        R.A, R.B, R.C = [k.ps([128, G, 128], stack=st) for _ in range(3)]
        Mh = st.enter_context(k.nc.psum_tensor(k.name("psM"), [128, 512], F32))
        R.Mlo = TL(Mh[:, 0:256], k.p.buf())
        R.Mhi = TL(Mh[:, 256:512], R.Mlo.buf)
        R.Cf = TL(R.C.base[:, :, :].rearrange("p c t -> p (c t)"), R.C.buf)
        f32t = lambda name: k.sb([128, TG], stack=st, name=name)
        (R.tA, R.tLW, R.tL, R.tPm, R.tSx, R.eKt, R.eNg, R.eR, R.eH, R.tKd, R.tB) = [f32t("f%d_%d" % (si, i)) for i in range(11)]
        R.tSf = f32t("tSf") if si == 1 else None
        R.gCs = [k.sb([128, G], stack=st) for _ in range(2)]
        R.Dts = [{n: k.sb([128, G, 128], BF16, stack=st, name=n) for n in ("KtD", "BtD", "KkD", "BhD", "KhD", "VD")} for _ in range(2)]
        for Dt_ in R.Dts:
            for n in Dt_:
                k.memset("gpsimd", Dt_[n][:, :, :], 0.0)
        R.RtSs = [k.sb([128, TG], BF16, stack=st) for _ in range(2)]
        R.tRts = [k.sb([128, TG], stack=st, name="tRt") for _ in range(2)]
        bt = lambda name: k.sb([128, G, 128], BF16, stack=st, name=name)
        R.Xs = [bt("X0"), bt("X1")]
        R.XTs = [bt("XT0"), bt("XT1")]
        R.Tms = [bt("T0"), bt("T1")]
        R.MkT, R.KtT, R.VT, R.BhT, R.KhT, R.WT, R.MkTt, R.U0T = [bt(n) for n in ("MkT", "KtT", "VT", "BhT", "KhT", "WT", "MkTt", "U0T")]
        R.NbS = k.sb([128, G, 64], BF16, stack=st)
        R.NkS = k.sb([128, G, 64], BF16, stack=st)
        R.P32 = k.sb([128, G, 128], stack=st)
        R.QT32 = k.sb([128, G, 128], stack=st)
        R.Rp32 = k.sb([128, G, 64], stack=st)
        R.Op32 = k.sb([128, G, 64], stack=st)
        R.STs = [k.sb([128, 128], stack=st, name="ST") for _ in range(2)]
        R.OT = [k.sb([128, TG], stack=st, name="OT")] * 2
        R.lor = [k.sb([128, 2, TG], BF16, stack=st, name="lor") for _ in range(2)]
        R.e1, R.e2 = ("vector", "gpsimd") if si == 0 else ("gpsimd", "vector")
        return R

    RS = [mk_stream(0), mk_stream(1)]
    Rr, Kr, Vr, KKr = Rr, Kr, Vr, KKr
    bc = lambda slot, w=128: TV(cm.base[:, slot:slot + 1, 0:w].to_broadcast([128, G, w]), cm.buf)
    g3 = lambda tl: TV(tl.base[:, :].rearrange("p (c t) -> p c t", c=G), tl.buf)

    def blkD(eng, dst, src_a, src_b, op=ALU.mult):
        for hh in range(2):
            ps_ = slice(hh * 64, hh * 64 + 64)
            o = TV(dst.base[ps_, :, hh * 64:hh * 64 + 64], dst.buf)
            a = TV(src_a.ap[ps_, :].rearrange("p (c t) -> p c t", c=G), src_a.buf)
            if src_b is None:
                k.copy(eng, o, a)
            else:
                b = TV(src_b.ap[ps_, :].rearrange("p (c t) -> p c t", c=G), src_b.buf)
                k.tt(eng, o, a, b, op)

    def sweep_gen(fc, d, R):
        e1, e2 = R.e1, R.e2
        k.memset("gpsimd", R.STs[0][:, :], 0.0)
        sti = 0
        NG = T // TG
        olist = list(range(NG) if d == 0 else range(NG - 1, -1, -1))
        nM, nMT, mI = (C_NSU, C_NSL, C_IU) if d == 0 else (C_NSL, C_NSU, C_IL)
        mSL = C_SL if d == 0 else C_SU
        tA, tLW, tL, tPm, tSx, tSf, eKt, eNg, eR, eH, tKd, tB = (R.tA, R.tLW, R.tL, R.tPm, R.tSx, R.tSf, R.eKt, R.eNg, R.eR, R.eH, R.tKd, R.tB)
        assert d == 0 or tSf is not None

        def prep_gen(n, sl):
            tsl = slice(n * TG, (n + 1) * TG)
            lor = R.lor[sl]
            Dt = R.Dts[sl]
            tRt = R.tRts[sl]
            gC = R.gCs[sl]
            KtD, BtD, KkD, BhD, KhD, VD = [Dt[nm] for nm in ("KtD", "BtD", "KkD", "BhD", "KhD", "VD")]
            k.dma(lor[:, 0, :], TV(LORd.base[d, :, tsl], LORd.buf))
            k.dma(lor[:, 1, :], TV(LORd.base[2 + d, :, tsl], LORd.buf))
            yield
            k.mm(R.Mlo[:, :], ICL[:, d, fc, :], lor[:, 1, :])
            k.mm(R.Mhi[:, :], DEC[:, d, fc, :], lor[:, 0, :])
            k.act(tA[:, :], R.Mlo[:, :], AF.Sigmoid, bias=pf(P_A0F + d, fc))
            k.act(tLW[:, :], R.Mhi[:, :], AF.Sigmoid, bias=pf(P_W0F + d, fc))
            yield
            k.ts(e2, tLW[:, :], tLW[:, :], -math.exp(-0.5), None, ALU.mult)
            k.scan(tL[:, :], m01[:, :], tLW[:, :], 0.0, ALU.mult, ALU.add)
            L3 = g3(tL)
            Ltot = TV(tL.base[:, :].rearrange("p (c t) -> p c t", c=G)[:, :, CH - 1:CH], tL.buf)
            gC3 = TV(gC.base[:, :].rearrange("p (c o) -> p c o", o=1), gC.buf)
            k.act(gC3, Ltot, AF.Exp)
            k.tt(e2, tPm[:, :], tL[:, :], tLW[:, :], ALU.subtract)
            k.tt(e1, g3(tSx), TV(Ltot.ap.to_broadcast([128, G, CH]), tL.buf), L3, ALU.subtract)
            yield
            if d == 0:
                k.act(eKt[:, :], tPm[:, :], AF.Exp)
                k.act(eNg[:, :], tL[:, :], AF.Exp, scale=-1.0)
                k.act(eR[:, :], tL[:, :], AF.Exp)
                k.act(eH[:, :], tSx[:, :], AF.Exp)
            else:
                k.tt(e2, tSf[:, :], tSx[:, :], tLW[:, :], ALU.add)
                k.act(eKt[:, :], tSx[:, :], AF.Exp)
                k.act(eNg[:, :], tSf[:, :], AF.Exp, scale=-1.0)
                k.act(eR[:, :], tSf[:, :], AF.Exp)
                k.act(eH[:, :], tPm[:, :], AF.Exp)
            k.ts(e1, tKd[:, :], tA[:, :], pf(P_KA, fc), omka[:, fc:fc + 1], ALU.mult, ALU.add)
            k.tt(e2, tKd[:, :], tKd[:, :], Kr[:, tsl], ALU.mult)
            k.tt(e2, tB[:, :], KKr[:, tsl], tA[:, :], ALU.mult)
            yield
            blkD(e1, KtD, KKr[:, tsl], eKt[:, :])
            blkD(e2, BtD, tB[:, :], eNg[:, :])
            yield
            blkD(e1, KkD, tKd[:, :], eNg[:, :])
            blkD(e2, BhD, tB[:, :], eH[:, :])
            yield
            blkD(e1, KhD, tKd[:, :], eH[:, :])
            blkD(e2, VD, Vr[:, tsl], None)
            k.tt(e1, tRt[:, :], Rr[:, tsl], eR[:, :], ALU.mult)
            k.copy(e2, R.RtSs[sl][:, :], tRt[:, :])
            yield

        def step(pg):
            if pg is not None:
                next(pg, None)

        for _ in prep_gen(olist[0], 0):
            pass
        for it, n in enumerate(olist):
            tsl = slice(n * TG, (n + 1) * TG)
            sl = it % 2
            OT = R.OT[0]
            Dt = R.Dts[sl]
            tRt = R.tRts[sl]
            gC = R.gCs[sl]
            RtS = R.RtSs[sl]
            KtD, BtD, KkD, BhD, KhD, VD = [Dt[nm] for nm in ("KtD", "BtD", "KkD", "BhD", "KhD", "VD")]
            pg = prep_gen(olist[it + 1], (it + 1) % 2) if it + 1 < len(olist) else None
            for c in range(G):
                k.mm(R.A[:, c, :], BtD[:, c, :], KtD[:, c, :])
            for c in range(G):
                k.mm(R.B[:, c, :], KtD[:, c, :], BtD[:, c, :])
            for c in range(G):
                k.mm(R.C[:, c, :], KtD[:, c, :], KkD[:, c, :])
            X, XT, Tm = R.Xs[0], R.XTs[0], R.Tms[0]
            k.tt("vector", X[:, :, :], R.A[:, :, :], bc(nM), ALU.mult)
            k.tt("vector", XT[:, :, :], R.B[:, :, :], bc(nMT), ALU.mult)
            k.tt("vector", R.MkT[:, :, :], R.C[:, :, :], bc(mSL), ALU.mult)
            k.tt("gpsimd", Tm[:, :, :], X[:, :, :], bc(C_ID), ALU.add)
            yield
            for c in range(G):
                k.mm(R.Mlo[:, c * 64:(c + 1) * 64], BtD[:, c, :], RtS[:, c * 64:(c + 1) * 64])
            for c in range(G):
                k.mm(R.Mhi[:, c * 64:(c + 1) * 64], KkD[:, c, :], RtS[:, c * 64:(c + 1) * 64])
            k.tt("vector", R.NbS[:, :, :], g3(R.Mlo), bc(mI, 64), ALU.mult)
            k.tt("vector", R.NkS[:, :, :], g3(R.Mhi), bc(mI, 64), ALU.mult)
            yield
            for i in range(1, 6):
                Xn, XTn, Tn = R.Xs[i % 2], R.XTs[i % 2], R.Tms[i % 2]
                if i < 5:
                    for c in range(G):
                        k.mm(R.A[:, c, :], XT[:, c, :], X[:, c, :])
                for c in range(G):
                    k.mm(R.B[:, c, :], X[:, c, :], XT[:, c, :])
                if i < 5:
                    k.copy("scalar", Xn[:, :, :], R.A[:, :, :])
                k.copy("vector", XTn[:, :, :], R.B[:, :, :])
                step(pg)
                yield
                for c in range(G):
                    k.mm(R.C[:, c, :], XTn[:, c, :], Tm[:, c, :], start=True, stop=False)
                    k.mm(R.C[:, c, :], idb[:, :], Tm[:, c, :], start=False, stop=True)
                k.copy("scalar", Tn[:, :, :], R.C[:, :, :])
                X, XT, Tm = Xn, XTn, Tn
                step(pg)
                yield
            for src, dst, ce, pb_ in ((KtD, R.KtT, "vector", R.A), (VD, R.VT, "scalar", R.B), (BhD, R.BhT, "vector", R.C), (KhD, R.KhT, "scalar", R.A)):
                for c in range(G):
                    k.mm(pb_[:, c, :], src[:, c, :], idb[:, :])
                k.copy(ce, dst[:, :, :], pb_[:, :, :])
            yield
            for c in range(G):
                k.mm(R.A[:, c, :], Tm[:, c, :], R.KtT[:, c, :])
            for c in range(G):
                k.mm(R.B[:, c, :], R.MkT[:, c, :], Tm[:, c, :])
            k.act(R.WT[:, :, :], R.A[:, :, :], AF.Copy, scale=-1.0)
            k.copy("vector", R.MkTt[:, :, :], R.B[:, :, :])
            yield
            for c in range(G):
                k.mm(R.C[:, c, :], R.MkTt[:, c, :], R.VT[:, c, :])
            for c in range(G):
                k.mm(R.A[:, c, :], R.WT[:, c, :], R.BhT[:, c, :])
            for c in range(G):
                k.mm(R.Mlo[:, c * 64:(c + 1) * 64], R.WT[:, c, :], R.NbS[:, c, :])
            k.act(R.U0T[:, :, :], R.C[:, :, :], AF.Copy, scale=-1.0)
            gCb = TV(gC.base[:, :].rearrange("p (c o) -> p c o", o=1).to_broadcast([128, G, 128]), gC.buf)
            for c in range(G):
                k.stt(R.P32[:, c, :], cm[:, C_ID, :], gC[:, c:c + 1], R.A[:, c, :], ALU.mult, ALU.add)
            k.tt("vector", R.Rp32[:, :, :], g3(R.Mlo), g3(tRt), ALU.add)
            yield
            for c in range(G):
                k.mm(R.B[:, c, :], R.BhT[:, c, :], R.U0T[:, c, :], start=True, stop=False)
                k.mm(R.B[:, c, :], R.KhT[:, c, :], R.VT[:, c, :], start=False, stop=True)
            for c in range(G):
                k.mm(R.Mhi[:, c * 64:(c + 1) * 64], R.U0T[:, c, :], R.NbS[:, c, :], start=True, stop=False)
                k.mm(R.Mhi[:, c * 64:(c + 1) * 64], R.VT[:, c, :], R.NkS[:, c, :], start=False, stop=True)
            k.copy("scalar", R.QT32[:, :, :], R.B[:, :, :])
            k.copy("scalar", R.Op32[:, :, :], g3(R.Mhi))
            yield
            corder = range(G) if d == 0 else range(G - 1, -1, -1)
            for c in corder:
                ST = R.STs[sti % 2]
                STn = R.STs[(sti + 1) % 2]
                sti += 1
                k.mm(R.Cf[:, c * 64:(c + 1) * 64], ST[:, :], R.Rp32[:, c, :])
                k.mm(R.Cf[:, 256:384], R.P32[:, c, :], ST[:, :])
                k.tt("vector", STn[:, :], R.Cf[:, 256:384], R.QT32[:, c, :], ALU.add)
                yield
            k.tt("vector", OT[:, :], R.Cf[:, 0:TG], TV(R.Op32.base[:, :, :].rearrange("p c t -> p (c t)"), R.Op32.buf), ALU.add)
            k.dma(TV(OD[d].base[:, tsl], OD[d].buf), OT[:, :])
            if pg is not None:
                for _ in pg:
                    pass
            yield

    for fc in range(8):
        R0 = RS[0]
        for t in range(T // SW):
            tsl = slice(t * SW, (t + 1) * SW)
            shift_tile(fc, t * SW, SW, Rr[:, tsl])
            shift_tile(8 + fc, t * SW, SW, Kr[:, tsl])
            shift_tile(16 + fc, t * SW, SW, Vr[:, tsl])
        for n in range(T // 512):
            tsl = slice(n * 512, (n + 1) * 512)
            pX, pY, pB, pK = PP[0], PP[1], PP[2], PP[3]
            if l == 1:
                k.dma(lorp[:, 0, :], TV(DNd.base[:, tsl], DNd.buf))
                for hh in range(2):
                    k.mm((R0.Mlo, R0.Mhi)[hh][:, :], VUP[:, fc, :], lorp[:, 0, hh * 256:(hh + 1) * 256])
                    k.act(pX[:, hh * 256:(hh + 1) * 256], (R0.Mlo, R0.Mhi)[hh][:, :], AF.Sigmoid, bias=pf(P_V0, fc))
                k.dma(pY[:, :], TV(vf_in.base[fc, :, tsl], vf_in.buf))
                k.tt("gpsimd", pY[:, :], pY[:, :], Vr[:, tsl], ALU.subtract)
                k.tt("vector", pY[:, :], pY[:, :], pX[:, :], ALU.mult)
                k.tt("gpsimd", Vr[:, tsl], Vr[:, tsl], pY[:, :], ALU.add)
            k.ts("vector", KKr[:, tsl], Kr[:, tsl], pf(P_KK, fc), None, ALU.mult)
            k.act(pB[:, :], KKr[:, tsl], AF.Square)
            for hh in range(2):
                k.mm((R0.Mlo, R0.Mhi)[hh][:, :], cm[:, C_B1, :], pB[:, hh * 256:(hh + 1) * 256])
                k.ts("vector", pK[:, hh * 256:(hh + 1) * 256], (R0.Mlo, R0.Mhi)[hh][:, :], 1e-24, None, ALU.max)
            k.act(pK[:, :], pK[:, :], AF.Sqrt)
            k.recip(pK[:, :], pK[:, :])
            k.tt("gpsimd", KKr[:, tsl], KKr[:, tsl], pK[:, :], ALU.mult)
        if l == 0:
            for t in range(4):
                tsl = slice(t * 1024, (t + 1) * 1024)
                k.dma(TV(vf_out.base[fc, :, tsl], vf_out.buf), Vr[:, tsl])
        for _ in itertools.zip_longest(sweep_gen(fc, 0, RS[0]), sweep_gen(fc, 1, RS[1])):
            pass
        for n in range(T // 512):
            tsl = slice(n * 512, (n + 1) * 512)
            R = RS[n % 2]
            pO, pF, pX, pY, pA2, pW = PP
            k.dma(pO[:, :], TV(OD[0].base[:, tsl], OD[0].buf))
            k.dma(pF[:, :], TV(OD[1].base[:, tsl], OD[1].buf))
            k.dma(lorp[:, :, :], TV(LORd.base[2:6, :, tsl].rearrange("r p t -> p r t"), LORd.buf))
            k.tt("gpsimd", pO[:, :], pO[:, :], pF[:, :], ALU.add)
            for hh in range(2):
                hs = slice(hh * 256, (hh + 1) * 256)
                k.mm(R.Mlo[:, :], ICL[:, 0, fc, :], lorp[:, 0, hs])
                k.mm(R.Mhi[:, :], ICL[:, 1, fc, :], lorp[:, 1, hs])
                k.act(pX[:, hs], R.Mlo[:, :], AF.Sigmoid, bias=pf(P_A0F, fc))
                k.act(pA2[:, hs], R.Mhi[:, :], AF.Sigmoid, bias=pf(P_A0B, fc))
            k.tt("gpsimd", pX[:, :], pX[:, :], pA2[:, :], ALU.add)
            k.ts("vector", pX[:, :], pX[:, :], pf(P_KA, fc), k.omka2[:, fc:fc + 1], ALU.mult, ALU.add)
            k.tt("gpsimd", pX[:, :], pX[:, :], Kr[:, tsl], ALU.mult)
            k.stt(pX[:, :], pX[:, :], pf(P_RK, fc), Rr[:, tsl], ALU.mult, ALU.mult)
            for hh in range(2):
                hs = slice(hh * 256, (hh + 1) * 256)
                k.mm((R.Mlo, R.Mhi)[hh][:, :], cm[:, C_B64, :], pO[:, hs])
                k.tt("vector", pO[:, hs], pO[:, hs], (R.Mlo, R.Mhi)[hh][:, :], ALU.subtract)
            k.act(pY[:, :], pO[:, :], AF.Square)
            for hh in range(2):
                hs = slice(hh * 256, (hh + 1) * 256)
                k.mm((R.Mlo, R.Mhi)[hh][:, :], cm[:, C_B64, :], pY[:, hs])
                k.act(pW[:, hs], (R.Mlo, R.Mhi)[hh][:, :], AF.Sqrt, bias=k.eps_gn[:, :])
            k.recip(pW[:, :], pW[:, :])
            k.tt("vector", pO[:, :], pO[:, :], pW[:, :], ALU.mult)
            k.act(pO[:, :], pO[:, :], AF.Identity, bias=pf(P_LNB, fc), scale=pf(P_LNW, fc))
            for hh in range(2):
                hs = slice(hh * 256, (hh + 1) * 256)
                k.mm((R.Mlo, R.Mhi)[hh][:, :], cm[:, C_B1, :], pX[:, hs])
                k.tt("vector", pY[:, hs], (R.Mlo, R.Mhi)[hh][:, :], Vr[:, n * 512 + hh * 256:n * 512 + (hh + 1) * 256], ALU.mult)
            k.tt("gpsimd", pO[:, :], pO[:, :], pY[:, :], ALU.add)
            for hh in range(2):
                hs = slice(hh * 256, (hh + 1) * 256)
                M_ = (R.Mlo, R.Mhi)[hh]
                k.mm(M_[:, :], GAT[:, 0, fc, :], lorp[:, 2, hs], start=True, stop=False)
                k.mm(M_[:, :], GAT[:, 1, fc, :], lorp[:, 3, hs], start=False, stop=True)
                k.tt("vector", pF[:, hs], pO[:, hs], M_[:, :], ALU.mult)
            k.dma(TV(yaT.base[fc, :, tsl], yaT.buf), pF[:, :])
    k.barrier()
    st.close()

    st = ExitStack()
    cm = k.sb([128, NCM, 128], stack=st)
    k.dma(cm[:, :, :], cmat[:, :, :], eng="sync")
    cosS = k.sb([128, T], stack=st)
    sinS = k.sb([128, T], stack=st)
    k.dma(cosS[:, :], cosT[:, :], eng="sync")
    k.dma(sinS[:, :], sinT[:, :], eng="gpsimd")
    lv = k.sb([128, 4, 128], stack=st)
    k.dma(lv[:, :, :], lamv[:, :, :], eng="sync")
    sw = k.sb([128, 256], stack=st)
    k.dma(sw[:, :], subw[:, :], eng="sync")
    k.ts("vector", sw[:, :], sw[:, :], float(1.0 - lam_init), None, ALU.mult)
    lt1 = k.sb([128, 2, 128], stack=st)
    ls = k.sb([128, 2], stack=st)
    k.tt("vector", lt1[:, 0, :], lv[:, 0, :], lv[:, 1, :], ALU.mult)
    k.tt("vector", lt1[:, 1, :], lv[:, 2, :], lv[:, 3, :], ALU.mult)
    k.p.op("vector", lambda e: e.tensor_reduce(out=ls[:, :].ap, in_=lt1[:, :, :].ap, axis=AX.X, op=ALU.add),
           reads=[lt1.buf], writes=[ls.buf])
    k.act(ls[:, :], ls[:, :], AF.Exp)
    nlam = k.sb([128, 1], stack=st)
    k.tt("vector", nlam[:, :], ls[:, 1:2], ls[:, 0:1], ALU.subtract)
    k.ts("vector", nlam[:, :], nlam[:, :], float(-lam_init), None, ALU.add)
    eps_s = k.sb([128, 1], stack=st)
    k.memset("gpsimd", eps_s[:, :], 1e-5)
    QK = [k.sb([128, T], BF16, stack=st, name="qk%d" % i) for i in range(4)]
    V1 = k.sb([128, T // 128, 260], BF16, stack=st, name="V1")
    k.memset("gpsimd", V1[:, :, 256:257], 1.0)
    zl = [k.sb([128, 512], stack=st, name="zl") for _ in range(2)]
    r1 = [k.sb([128, 512], stack=st, name="r1") for _ in range(2)]
    r2 = [k.sb([128, 512], stack=st, name="r2") for _ in range(2)]
    pT = [k.sb([128, 256], BF16, stack=st, name="pT") for _ in range(3)]
    Oacc = [[k.ps([128, 512], stack=st) for _ in range(2)] for _ in range(2)]
    SP = [k.ps([128, 512], stack=st) for _ in range(2)]
    PTf = k.ps([128, 4, 128], stack=st)
    PR = k.ps([128, 512], stack=st)
    osb = [k.sb([128, 256], stack=st, name="osb") for _ in range(2)]
    tsb = [k.sb([128, 256], stack=st, name="tsb") for _ in range(2)]
    sm = [k.sb([128, 8], stack=st, name="sm") for _ in range(2)]
    yrow = [k.sb([128, 2, 256], stack=st, name="yrow") for _ in range(2)]
    scale = 128 ** -0.5
    it = 0
    for hd in range(4):
        rows = (30 + 2 * hd, 31 + 2 * hd, 38 + 2 * hd, 39 + 2 * hd)
        for ri, j in enumerate(rows):
            for t in range(8):
                tsl = slice(t * 512, (t + 1) * 512)
                z = zl[it % 2]
                a, b_ = r1[it % 2], r2[it % 2]
                it += 1
                k.dma(z[:, :], zrow(j, 1 + t * 512, 1 + (t + 1) * 512))
                k.mm(PR[:, :], cm[:, C_ROT, :], z[:, :])
                k.tt("gpsimd", a[:, :], z[:, :], cosS[:, tsl], ALU.mult)
                k.tt("vector", b_[:, :], PR[:, :], sinS[:, tsl], ALU.mult)
                k.tt("gpsimd", QK[ri][:, tsl], a[:, :], b_[:, :], ALU.add)
        for ec in range(2):
            j = 46 + 2 * hd + ec
            for t in range(8):
                z = zl[it % 2]
                it += 1
                k.dma(z[:, :], zrow(j, 1 + t * 512, 1 + (t + 1) * 512))
                for q in range(4):
                    k.tr(PTf[:, q, :], z[:, q * 128:(q + 1) * 128], cm[:, C_ID, :])
                k.copy("vector", V1[:, t * 4:(t + 1) * 4, ec * 128:(ec + 1) * 128], PTf[:, :, :])
        for qt in range(T // 256):
            qsl = slice(qt * 256, (qt + 1) * 256)
            steps = [(kt, m) for kt in range(T // 128) for m in range(2)]
            NKT = T // 128

            def S_(i):
                kt, m = steps[i]
                k.mm(SP[i % 2][:, 0:256], QK[2 + m][:, kt * 128:(kt + 1) * 128], QK[m][:, qsl])

            def EPV_(i):
                kt, m = steps[i]
                p_ = pT[i % 3]
                k.act(p_[:, :], SP[i % 2][:, 0:256], AF.Exp, scale=float(scale))
                for qb in range(2):
                    k.mm(Oacc[m][qb][:, 0:257], p_[:, qb * 128:(qb + 1) * 128], V1[:, kt, 0:257],
                         start=(kt == 0), stop=(kt == NKT - 1))

            S_(0)
            for i in range(len(steps)):
                if i + 1 < len(steps):
                    S_(i + 1)
                EPV_(i)
            yr = yrow[qt % 2]
            for qb in range(2):
                s_ = sm[qb]
                o_, t_ = osb[qb], tsb[qb]
                k.recip(s_[:, 0:1], Oacc[0][qb][:, 256:257])
                k.recip(s_[:, 1:2], Oacc[1][qb][:, 256:257])
                k.tt("vector", s_[:, 2:3], s_[:, 1:2], nlam[:, :], ALU.mult)
                k.ts("vector", t_[:, :], Oacc[1][qb][:, 0:256], s_[:, 2:3], None, ALU.mult)
                k.stt(o_[:, :], Oacc[0][qb][:, 0:256], s_[:, 0:1], t_[:, :], ALU.mult, ALU.add)
                k.act(t_[:, :], o_[:, :], AF.Square, accum=s_[:, 3:4])
                k.act(s_[:, 4:5], s_[:, 3:4], AF.Sqrt, bias=eps_s[:, :], scale=1.0 / 256)
                k.recip(s_[:, 5:6], s_[:, 4:5])
                k.stt(o_[:, :], o_[:, :], s_[:, 5:6], sw[:, :], ALU.mult, ALU.mult)
                for ec in range(2):
                    k.tr(PTf[:, ec, :], o_[:, ec * 128:(ec + 1) * 128], cm[:, C_ID, :])
                k.copy("scalar", yr[:, :, qb * 128:(qb + 1) * 128], PTf[:, 0:2, :])
            for ec in range(2):
                k.dma(TV(ybT.base[2 * hd + ec, :, qsl], ybT.buf), yr[:, ec, :])
    k.barrier()
    st.close()


def pack_A(inp, l, core, consts, mods, vfirst=None):
    b, h = core // 2, core % 2
    cm, m01, cosT, sinT = consts
    W = inp["w_in"][l]
    R0 = 0
    cols = []
    f0 = h * 1024
    for base in (0, 2048, 4096):
        cols.append(W[:, base + f0:base + f0 + 1024])
    lo = 6144

    def pad(a, n=128):
        return np.concatenate([a, np.zeros((a.shape[0], n - a.shape[1]), np.float32)], 1)
    for i in range(4):
        cols.append(pad(W[:, lo + 96 * i:lo + 96 * (i + 1)]))
    cols.append(W[:, lo + 384:lo + 640])
    q0 = 6784
    for base in (q0, q0 + 2048, q0 + 4096):
        cols.append(W[:, base + f0:base + f0 + 1024])
    if l == 1:
        cols.append(pad(inp["vres_down"][0]))
    Wc = np.ascontiguousarray(np.concatenate(cols, 1))
    vec = np.zeros((128, NV), np.float32)
    mp, mn = inp["shift_mu_prev"][l], inp["shift_mu_next"][l]

    def rowvec(v):
        out = np.zeros((128, 30), np.float32)
        for blk in range(3):
            out[:, blk * 8:(blk + 1) * 8] = v[blk * 2048 + f0:blk * 2048 + f0 + 1024].reshape(8, 128).T
        for i in range(4):
            out[0:96, 24 + i] = v[lo + 96 * i:lo + 96 * (i + 1)]
        out[:, 28:30] = v[lo + 384:lo + 640].reshape(2, 128).T
        return out
    vec[:, V_MP:V_MP + 30] = rowvec(mp)
    vec[:, V_MN:V_MN + 30] = rowvec(mn)
    my = lambda v: np.asarray(v, np.float32).reshape(-1)[f0:f0 + 1024].reshape(8, 128).T
    plist = [inp["k_k"][l], inp["k_a"][l], inp["r_k"][l], inp["ln_x_w"][l], inp["ln_x_b"][l],
             inp["decay_w0"][l][0], inp["decay_w0"][l][1], inp["iclr_a0"][l][0], inp["iclr_a0"][l][1],
             inp["vres_v0"][0] if l == 1 else np.zeros(2048, np.float32)]
    for i, v in enumerate(plist):
        vec[:, V_PF + 8 * i:V_PF + 8 * i + 8] = my(v)

    def lora_w(w2):
        out = np.zeros((128, 2, 8, 128), np.float32)
        for d in range(2):
            out[0:96, d] = w2[d][:, f0:f0 + 1024].reshape(96, 8, 128)
        return out
    lwd = lora_w(inp["decay_up"][l])
    lwi = lora_w(inp["iclr_up"][l])
    lwg = np.ascontiguousarray(inp["gate_up"][l][:, f0:f0 + 1024].reshape(2, 128, 8, 128).transpose(1, 0, 2, 3))
    lwv = np.zeros((128, 8, 128), np.float32)
    if l == 1:
        lwv[0:64] = inp["vres_up"][0][:, f0:f0 + 1024].reshape(64, 8, 128)
    lamv = np.ascontiguousarray(np.broadcast_to(np.stack([inp[n][l] for n in ("lambda_q1", "lambda_k1", "lambda_q2", "lambda_k2")])[None], (128, 4, 128)))
    subw = np.ascontiguousarray(np.broadcast_to(inp["subln_w"][l][None], (128, 256)))
    d = {"xT": mods["xT"][b], "modT": mods["mod"][b][l], "Wc": Wc, "vec": vec, "lwd": lwd, "lwi": lwi, "lwg": lwg, "lwv": lwv,
         "lamv": lamv, "subw": subw, "cosT": cosT, "sinT": sinT, "cmat": cm, "m01": m01}
    if l == 1:
        d["vf_in"] = vfirst[core]
    return d


def build_A(l):
    nc = bass.Bass("TRN2", target_bir_lowering=False)
    NRW = 54 + (1 if l == 1 else 0)
    with ExitStack() as st:
        k = K(nc, st)
        IN = lambda n, s: k.dram(n, s, kind="ExternalInput")
        xT = IN("xT", [KC, 128, T])
        modT = IN("modT", [128, 96])
        Wc = IN("Wc", [D, NRW * 128])
        vec = IN("vec", [128, NV])
        lwd, lwi, lwg = [IN(n, [128, 2, 8, 128]) for n in ("lwd", "lwi", "lwg")]
        lwv = IN("lwv", [128, 8, 128])
        lamv = IN("lamv", [128, 4, 128])
        subw = IN("subw", [128, 256])
        cosT, sinT = IN("cosT", [128, T]), IN("sinT", [128, T])
        cmat = IN("cmat", [128, NCM, 128])
        m01 = IN("m01", [128, TG])
        vf_in = IN("vf_in", [8, 128, T]) if l == 1 else None
        yaT = k.dram("yaT", [8, 128, T], kind="ExternalOutput")
        ybT = k.dram("ybT", [8, 128, T], kind="ExternalOutput")
        vf_out = k.dram("vf_out", [8, 128, T], kind="ExternalOutput") if l == 0 else None
        Zs = k.dram("Zs", [NRW, 128, T + 2], kind="Internal")
        ZsB = [k.p.buf() for _ in range(NRW)]
        xsrc = lambda i: [(slice(0, KC), TV(xT.base[:, :, i * 128:(i + 1) * 128].rearrange("c p t -> p c t"), xT.buf))]
        emit_phaseA(k, l, xsrc, modT, Wc, vec, lwd, lwi, lwg, lwv, lamv, subw, cosT, sinT, cmat, m01, vf_in, yaT, ybT, vf_out, Zs, ZsB)
        k.p.finish()
        k.p.emit()
    return nc


_CACHE = {}


def _prog(name, fn):
    if name not in _CACHE:
        _CACHE[name] = fn()
    return _CACHE[name]


def _run(nc, ims):
    return run_bass_kernel_spmd(nc, ims, core_ids=list(range(8))).results


def pack_B(inp, l, core, xT_full, ya, yb, mod):
    b, h = core // 2, core % 2
    tok = slice(h * NT, (h + 1) * NT)
    W = inp["w_in"][l]
    return {"xT": np.ascontiguousarray(xT_full[b][:, :, tok]),
            "yaT": np.ascontiguousarray(ya[b][:, :, tok]), "ybT": np.ascontiguousarray(yb[b][:, :, tok]),
            "wga": np.ascontiguousarray(W[:, 12928:14976]), "wgb": np.ascontiguousarray(W[:, 14976:17024]),
            "modT": mod[b][l],
            "lnv": np.ascontiguousarray(np.stack([_fm(inp[n][l], 16) for n in ("ln1_g", "ln1_b", "ln2_g", "ln2_b")], 1)),
            "proj_a": inp["proj_a"][l], "proj_b": inp["proj_b"][l], "w_out": inp["w_out"][l],
            "wg": inp["ffn_w_gate"][l], "wu": inp["ffn_w_up"][l], "wd": inp["ffn_w_down"][l]}


def kernel(**inputs):
    inp = {k_: np.asarray(v, np.float32) for k_, v in inputs.items()}
    consts = host_consts()
    ncM = _prog("M", build_M)
    ada_bT = np.stack([_fm(inp["ada_b"][ll], 96) for ll in range(2)])
    resM = _run(ncM, [{"cT": _fm(inp["c"][core // 2], 16), "ada_w": inp["ada_w"], "ada_bT": ada_bT} for core in range(8)])
    mod = [[np.ascontiguousarray(resM[2 * b]["mod"][ll]) for ll in range(2)] for b in range(4)]
    x_cur = inp["x"]
    vfirst = None
    ncB = _prog("B", build_B)
    for l in range(2):
        xT_full = [np.ascontiguousarray(x_cur[b].T.reshape(KC, 128, T)) for b in range(4)]
        ncA = _prog("A%d" % l, lambda: build_A(l))
        mods = {"xT": xT_full, "mod": mod}
        resA = _run(ncA, [pack_A(inp, l, core, consts, mods, vfirst) for core in range(8)])
        ya = [np.concatenate([resA[2 * b]["yaT"], resA[2 * b + 1]["yaT"]], 0) for b in range(4)]
        yb = [np.concatenate([resA[2 * b]["ybT"], resA[2 * b + 1]["ybT"]], 0) for b in range(4)]
        if l == 0:
            vfirst = [resA[c]["vf_out"] for c in range(8)]
        resB = _run(ncB, [pack_B(inp, l, core, xT_full, ya, yb, mod) for core in range(8)])
        x_next = np.empty_like(x_cur)
        for core in range(8):
            b, h = core // 2, core % 2
            x_next[b, h * NT:(h + 1) * NT, :] = resB[core]["x_out"].reshape(D, NT).T
        x_cur = x_next
    return x_cur


A_NAMES = ("Wc", "vec", "lwd", "lwi", "lwg", "lwv", "lamv", "subw")
B_NAMES = ("lnv", "proj_a", "proj_b", "w_out", "wga", "wgb", "wg", "wu", "wd")


def build_fused():
    nc = bass.Bass("TRN2", target_bir_lowering=False)
    with ExitStack() as st:
        k = K(nc, st)
        IN = lambda n, s: k.dram(n, s, kind="ExternalInput")
        cT = IN("cT", [128, KC])
        ada_w = IN("ada_w", [2, D, 6 * D])
        ada_bT = IN("ada_bT", [2, 128, 96])
        xT = IN("xT", [KC, 128, T])
        xTo = IN("xTo", [KC, 128, NT])
        sel = IN("sel", [128, 2])
        cosT, sinT = IN("cosT", [128, T]), IN("sinT", [128, T])
        cmat = IN("cmat", [128, NCM, 128])
        m01 = IN("m01", [128, TG])
        A_in, B_in = [], []
        for l in range(2):
            NRW = 54 + l
            A_in.append(dict(Wc=IN("Wc%d" % l, [D, NRW * 128]), vec=IN("vec%d" % l, [128, NV]),
                             lwd=IN("lwd%d" % l, [128, 2, 8, 128]), lwi=IN("lwi%d" % l, [128, 2, 8, 128]),
                             lwg=IN("lwg%d" % l, [128, 2, 8, 128]), lwv=IN("lwv%d" % l, [128, 8, 128]),
                             lamv=IN("lamv%d" % l, [128, 4, 128]), subw=IN("subw%d" % l, [128, 256])))
            B_in.append(dict(lnv=IN("lnv%d" % l, [128, 4, KC]), proj_a=IN("proj_a%d" % l, [D, D]), proj_b=IN("proj_b%d" % l, [D, D]),
                             w_out=IN("w_out%d" % l, [D, D]), wga=IN("wga%d" % l, [D, D]), wgb=IN("wgb%d" % l, [D, D]),
                             wg=IN("wg%d" % l, [D, FH]), wu=IN("wu%d" % l, [D, FH]), wd=IN("wd%d" % l, [FH, D])))
        x_out = k.dram("x_out", [KC, 128, NT], kind="ExternalOutput")
        ITL = lambda n, s: TL(nc.dram_tensor(n, list(s), F32).ap(), k.p.buf())
        modD = ITL("modD", [2, 128, 96])
        Zs = ITL("Zs", [55, 128, T + 2])
        ZsB = [k.p.buf() for _ in range(55)]
        vf = ITL("vf", [8, 128, T])
        ya = [ITL("ya%d" % l, [8 * 128, T]) for l in range(2)]
        yb = [ITL("yb%d" % l, [8 * 128, T]) for l in range(2)]
        GA = [ITL("GA%d" % l, [8, 2 * 128, T]) for l in range(2)]
        GB = [ITL("GB%d" % l, [8, 2 * 128, T]) for l in range(2)]
        x1 = ITL("x1", [KC * 128, NT])
        GX = ITL("GX", [8, 2 * 256, NT])
        v3 = lambda tl, n: TL(tl.base.rearrange("(c p) t -> c p t", p=128), tl.buf)
        v4 = lambda tl: TL(tl.base.rearrange("c (r p) t -> r c p t", r=2), tl.buf)

        emit_mod(k, cT, ada_w, ada_bT, modD)
        for l in range(2):
            a, b = A_in[l], B_in[l]
            modT = TL(modD.base[l], modD.buf)
            if l == 0:
                xsrc = lambda i: [(slice(0, KC), TV(xT.base[:, :, i * 128:(i + 1) * 128].rearrange("c p t -> p c t"), xT.buf))]
                x_own = xTo
            else:
                gx5 = GX.base.rearrange("j (r cc p) t -> r j cc p t", r=2, cc=2)
                xsrc = lambda i: [(slice(cc, KC, 2), TV(gx5[i // 16, :, cc, :, (i % 16) * 128:(i % 16 + 1) * 128].rearrange("j p t -> p j t"), GX.buf))
                                  for cc in range(2)]
                x_own = v3(x1, KC)
            emit_phaseA(k, l, xsrc, modT, a["Wc"], a["vec"], a["lwd"], a["lwi"], a["lwg"], a["lwv"], a["lamv"], a["subw"],
                        cosT, sinT, cmat, m01, vf if l == 1 else None, v3(ya[l], 8), v3(yb[l], 8), vf if l == 0 else None, Zs, ZsB)
            k.allgather_pieces(GA[l], ya[l], 128)
            k.allgather_pieces(GB[l], yb[l], 128)
            xo = v3(x1, KC) if l == 0 else x_out
            emit_phase3(k, l, x_own, v4(GA[l]), v4(GB[l]), sel, b["wga"], b["wgb"], modT, b["lnv"], b["proj_a"], b["proj_b"], b["w_out"],
                        b["wg"], b["wu"], b["wd"], xo, None)
            if l == 0:
                k.allgather_pieces(GX, x1, 256)
        k.p.finish()
        k.p.emit()
    return nc


def kernel(**inputs):
    inp = {k_: np.asarray(v, np.float32) for k_, v in inputs.items()}
    consts = host_consts()
    cm, m01, cosT, sinT = consts
    nc = _prog("F", build_fused)
    ada_bT = np.stack([_fm(inp["ada_b"][ll], 96) for ll in range(2)])
    xT_full = [np.ascontiguousarray(inp["x"][b].T.reshape(KC, 128, T)) for b in range(4)]
    ims = []
    for core in range(8):
        b, h = core // 2, core % 2
        sel = np.zeros((128, 2), np.float32)
        sel[:, h] = 1.0
        d = {"cT": _fm(inp["c"][b], 16), "ada_w": inp["ada_w"], "ada_bT": ada_bT, "xT": xT_full[b],
             "xTo": np.ascontiguousarray(xT_full[b][:, :, h * NT:(h + 1) * NT]), "sel": sel,
             "cosT": cosT, "sinT": sinT, "cmat": cm, "m01": m01}
        for l in range(2):
            pa = pack_A(inp, l, core, consts, {"xT": xT_full, "mod": [[None, None]] * 4}, [None] * 8)
            for n in A_NAMES:
                d["%s%d" % (n, l)] = pa[n]
            W = inp["w_in"][l]
            d["lnv%d" % l] = np.ascontiguousarray(np.stack([_fm(inp[n][l], 16) for n in ("ln1_g", "ln1_b", "ln2_g", "ln2_b")], 1))
            d["proj_a%d" % l] = inp["proj_a"][l]
            d["proj_b%d" % l] = inp["proj_b"][l]
            d["w_out%d" % l] = inp["w_out"][l]
            d["wga%d" % l] = np.ascontiguousarray(W[:, 12928:14976])
            d["wgb%d" % l] = np.ascontiguousarray(W[:, 14976:17024])
            d["wg%d" % l] = inp["ffn_w_gate"][l]
            d["wu%d" % l] = inp["ffn_w_up"][l]
            d["wd%d" % l] = inp["ffn_w_down"][l]
        ims.append(d)
    res = _run(nc, ims)
    out = np.empty((4, T, D), np.float32)
    for core in range(8):
        b, h = core // 2, core % 2
        out[b, h * NT:(h + 1) * NT, :] = res[core]["x_out"].reshape(D, NT).T
    return out
```

```python
import numpy as np
import concourse.bass as bass
import concourse.mybir as mybir

F32 = mybir.dt.float32
BF16 = mybir.dt.bfloat16
AF = mybir.ActivationFunctionType
ALU = mybir.AluOpType
AX = mybir.AxisListType

ENGS = ("tensor", "vector", "scalar", "gpsimd", "sync")


class Buf:
    __slots__ = ("name", "lw", "rd", "chan")

    def __init__(self, name, chan=None):
        self.name = name
        self.lw = None
        self.rd = {}
        self.chan = chan


class Chan:
    __slots__ = ("key", "count", "unit")

    def __init__(self, key, unit=16):
        self.key = key
        self.count = 0
        self.unit = unit


class Prog:
    def __init__(self, nc, stack, nchan=64):
        self.nc = nc
        self.stack = stack
        self.sems = {}
        self.ops = {e: [] for e in ENGS}
        self.ecnt = {e: 0 for e in ENGS}
        self.seen = {e: {} for e in ENGS}
        for e in ENGS:
            self.sems["e_" + e] = stack.enter_context(nc.semaphore("se_" + e))
        self.chans = []
        for i in range(nchan):
            k = "c%d" % i
            self.sems[k] = stack.enter_context(nc.semaphore("sc%d" % i))
            self.chans.append(Chan(k))
        self._nextchan = 0
        self.chan_by_key = {c.key: c for c in self.chans}
        self.cchans = []
        self.cfree = []
        for i in range(1):
            kk = "cc%d" % i
            self.sems[kk] = stack.enter_context(nc.semaphore("s" + kk))
            c = Chan(kk, unit=1)
            self.chan_by_key[kk] = c
            self.cfree.append(c)
        self.nbuf = 0

    def chan(self):
        c = self.chans[self._nextchan % len(self.chans)]
        self._nextchan += 1
        return c

    def buf(self, name=None, chan=None):
        self.nbuf += 1
        return Buf(name or ("b%d" % self.nbuf), chan)

    def _need(self, eng, ev, waits):
        if ev is None:
            return
        k, v = ev
        if k in self.chan_by_key:
            v = self.chan_by_key[k].unit * self.chan_by_key[k].count
        if self.seen[eng].get(k, 0) >= v:
            return
        self.seen[eng][k] = v
        waits[k] = max(waits.get(k, 0), v)

    def coll_chan(self):
        c = self.cfree.pop(0)
        self.cchans.append(c)
        return c

    def op(self, eng, fn, reads=(), writes=(), acc=False, dma=False, coll=False):
        waits = {}
        own = "e_" + eng
        for b in reads:
            if b.lw is not None:
                self._need(eng, b.lw, waits)
        for b in writes:
            if b.lw is not None:
                k = b.lw[0]
                if not (k == own and not dma):
                    self._need(eng, b.lw, waits)
            for k, v in b.rd.items():
                if k == own and not dma:
                    continue
                self._need(eng, (k, v), waits)
        if coll:
            ch = self.cfree[0]
            if ch not in self.cchans:
                self.cchans.append(ch)
            ch.count += 1
            ev = (ch.key, ch.count)
            inc = (ch.key, None)
        elif dma:
            ch = None
            for b in writes:
                if b.chan is None:
                    b.chan = self.chan()
                ch = b.chan
            assert ch is not None
            ch.count += 1
            ev = (ch.key, 16 * ch.count)
            inc = (ch.key, 16)
        else:
            self.ecnt[eng] += 1
            ev = (own, self.ecnt[eng])
            inc = (own, 1)
            self.seen[eng][own] = max(self.seen[eng].get(own, 0), 0)
        for b in reads:
            b.rd[ev[0]] = max(b.rd.get(ev[0], 0), ev[1])
        for b in writes:
            b.lw = ev
            b.rd = {}
        self.ops[eng].append((tuple(waits.items()), fn, inc))

    def finish(self):
        waits = []
        for c in self.chans + self.cchans:
            if c.count:
                waits.append((c.key, c.unit * c.count))
        for e in ENGS:
            if e != "sync" and self.ecnt[e]:
                waits.append(("e_" + e, self.ecnt[e]))
        self.ops["sync"].append((tuple(waits), None, None))

    def emit(self):
        nc = self.nc
        sems = self.sems

        def run(e, name):
            for waits, fn, inc in self.ops[name]:
                for k, v in waits:
                    e.wait_ge(sems[k], v)
                if fn is not None:
                    ins = fn(e)
                    if inc[1] is None:
                        ins.then_inc(sems[inc[0]])
                    else:
                        ins.then_inc(sems[inc[0]], inc[1])

        with nc.Block() as block:
            @block.tensor
            def _(e):
                run(e, "tensor")

            @block.vector
            def _(e):
                run(e, "vector")

            @block.scalar
            def _(e):
                run(e, "scalar")

            @block.gpsimd
            def _(e):
                run(e, "gpsimd")

            @block.sync
            def _(e):
                run(e, "sync")


from contextlib import ExitStack
from concourse.bass_utils import run_bass_kernel_spmd

D = 2048
T = 4096
NT = 2048
KC = 16
FH = 5632
HC = 44
ALPHA = 4 ** 0.25
LN_EPS = 1e-5


class TV:
    __slots__ = ("ap", "buf")

    def __init__(self, ap, buf):
        self.ap = ap
        self.buf = buf


class TL:
    def __init__(self, base, buf):
        self.base = base
        self.buf = buf

    def __getitem__(self, idx):
        return TV(self.base[idx], self.buf)


class K:
    def __init__(self, nc, stack, nchan=80):
        self.nc = nc
        self.st = stack
        self.p = Prog(nc, stack, nchan=nchan)
        self.n = 0
        self.rr = 0

    def name(self, s):
        self.n += 1
        return "%s_%d" % (s, self.n)

    def sb(self, shape, dt=F32, stack=None, name="t"):
        h = (stack or self.st).enter_context(self.nc.sbuf_tensor(self.name(name), list(shape), dt))
        return TL(h, self.p.buf())

    def ps(self, shape=(128, 512), dt=F32, stack=None):
        h = (stack or self.st).enter_context(self.nc.psum_tensor(self.name("ps"), list(shape), dt))
        return TL(h, self.p.buf())

    def dram(self, name, shape, dt=F32, kind="Internal"):
        h = self.nc.dram_tensor(name, list(shape), dt, kind=kind)
        return TL(h.ap(), self.p.buf())

    def mm(self, out, lhsT, rhs, start=True, stop=True):
        self.p.op("tensor", lambda e: e.matmul(out.ap, lhsT=lhsT.ap, rhs=rhs.ap, start=start, stop=stop),
                  reads=[lhsT.buf, rhs.buf], writes=[out.buf])

    def tr(self, out, in_, ident):
        self.p.op("tensor", lambda e: e.transpose(out.ap, in_.ap, ident.ap),
                  reads=[in_.buf, ident.buf], writes=[out.buf])

    def act(self, out, in_, func, bias=None, scale=None, accum=None):
        rd = [in_.buf]
        kw = {}
        if bias is not None:
            if isinstance(bias, TV):
                rd.append(bias.buf)
                kw["bias"] = bias.ap
            else:
                kw["bias"] = bias
        if scale is not None:
            if isinstance(scale, TV):
                rd.append(scale.buf)
                kw["scale"] = scale.ap
            else:
                kw["scale"] = scale
        wr = [out.buf]
        if accum is not None:
            kw["accum_out"] = accum.ap
            wr.append(accum.buf)
        self.p.op("scalar", lambda e: e.activation(out=out.ap, in_=in_.ap, func=func, **kw), reads=rd, writes=wr)

    def tt(self, eng, out, a, b, op):
        self.p.op(eng, lambda e: e.tensor_tensor(out=out.ap, in0=a.ap, in1=b.ap, op=op),
                  reads=[a.buf, b.buf], writes=[out.buf])

    def ts(self, eng, out, a, s1, s2, op0, op1=None):
        rd = [a.buf]
        v1 = s1.ap if isinstance(s1, TV) else s1
        v2 = s2.ap if isinstance(s2, TV) else s2
        if isinstance(s1, TV):
            rd.append(s1.buf)
        if isinstance(s2, TV):
            rd.append(s2.buf)
        if op1 is None:
            self.p.op(eng, lambda e: e.tensor_scalar(out=out.ap, in0=a.ap, scalar1=v1, scalar2=None, op0=op0),
                      reads=rd, writes=[out.buf])
        else:
            self.p.op(eng, lambda e: e.tensor_scalar(out=out.ap, in0=a.ap, scalar1=v1, scalar2=v2, op0=op0, op1=op1),
                      reads=rd, writes=[out.buf])

    def stt(self, out, a, s, b, op0, op1):
        rd = [a.buf, b.buf]
        v = s.ap if isinstance(s, TV) else s
        if isinstance(s, TV):
            rd.append(s.buf)
        self.p.op("vector", lambda e: e.scalar_tensor_tensor(out=out.ap, in0=a.ap, scalar=v, in1=b.ap, op0=op0, op1=op1),
                  reads=rd, writes=[out.buf])

    def copy(self, eng, out, in_):
        if eng == "scalar":
            self.p.op(eng, lambda e: e.copy(out=out.ap, in_=in_.ap), reads=[in_.buf], writes=[out.buf])
        else:
            self.p.op(eng, lambda e: e.tensor_copy(out=out.ap, in_=in_.ap), reads=[in_.buf], writes=[out.buf])

    def memset(self, eng, out, val):
        self.p.op(eng, lambda e: e.memset(out.ap, val), writes=[out.buf])

    def recip(self, out, in_):
        self.p.op("vector", lambda e: e.reciprocal(out=out.ap, in_=in_.ap), reads=[in_.buf], writes=[out.buf])

    def scan(self, out, d0, d1, init, op0, op1):
        self.p.op("vector", lambda e: e.tensor_tensor_scan(out=out.ap, data0=d0.ap, data1=d1.ap, initial=init, op0=op0, op1=op1),
                  reads=[d0.buf, d1.buf], writes=[out.buf])

    def dma(self, out, in_, eng=None, slow=False):
        if eng is None or eng == "gpsimd":
            eng = "sync"
        if slow:
            self.p.op(eng, lambda e: e.dma_start(out=out.ap, in_=in_.ap, allow_slow_non_contiguous=True),
                      reads=[in_.buf], writes=[out.buf], dma=True)
        else:
            self.p.op(eng, lambda e: e.dma_start(out=out.ap, in_=in_.ap), reads=[in_.buf], writes=[out.buf], dma=True)

    def allgather(self, out, in_):
        groups = [[0, 1], [2, 3], [4, 5], [6, 7]]
        self.p.op("gpsimd", lambda e: e.collective_compute("AllGather", ALU.bypass, replica_groups=groups, ins=[in_.ap], outs=[out.ap]),
                  reads=[in_.buf], writes=[out.buf], coll=True)

    def allgather_pieces(self, G5, src, rows):
        n = src.base.shape[0] // rows
        for j in range(n):
            self.allgather(TV(G5.base[j], G5.buf), TV(src.base[j * rows:(j + 1) * rows, :], src.buf))

    def barrier(self):
        p = self.p
        for e in ENGS:
            waits = {}
            for c in p.chans + p.cchans:
                if c.count:
                    p._need(e, (c.key, c.unit * c.count), waits)
            for e2 in ENGS:
                if e2 != e and p.ecnt[e2]:
                    p._need(e, ("e_" + e2, p.ecnt[e2]), waits)
            if waits:
                p.ops[e].append((tuple(waits.items()), None, None))


def emit_mod(k, cT, ada_w, ada_bT, mod_out, nl=2):
    st = ExitStack()
    cs = k.sb([128, KC], stack=st)
    sc = k.sb([128, KC], stack=st)
    k.dma(cs[:, :], cT[:, :], eng="sync")
    k.act(sc[:, :], cs[:, :], AF.Silu)
    wst = [k.sb([128, KC, 512], stack=st, name="adaw") for _ in range(2)]
    ps = k.ps(stack=st)
    bt = k.sb([128, nl, 96], stack=st)
    res = k.sb([128, nl, 96], stack=st)
    for l in range(nl):
        k.dma(bt[:, l, :], ada_bT[l, :, :], eng="sync")
    g = 0
    for l in range(nl):
        for mg in range(24):
            w = wst[g % 2]
            g += 1
            for kh in range(4):
                k.dma(w[:, kh * 4:(kh + 1) * 4, :],
                      TV(ada_w.base[l, kh * 512:(kh + 1) * 512, mg * 512:(mg + 1) * 512].rearrange("(c p) n -> p c n", p=128), ada_w.buf))
            for mi in range(4):
                m = mg * 4 + mi
                for kc in range(KC):
                    k.mm(ps[:, l * 96 + m:l * 96 + m + 1], w[:, kc, mi * 128:(mi + 1) * 128], sc[:, kc:kc + 1],
                         start=(kc == 0), stop=(kc == KC - 1))
        k.tt("vector", res[:, l, :], ps[:, l * 96:(l + 1) * 96], bt[:, l, :], ALU.add)
        k.dma(mod_out[l, :, :], res[:, l, :], eng="sync")
    k.barrier()
    st.close()


class WStream:
    def __init__(self, k, stack, kmax=22, nbuf=3):
        self.k = k
        self.stg = [k.sb([128, kmax, 128], stack=stack, name="wstg") for _ in range(2)]
        self.wb = [k.sb([128, kmax, 128], BF16, stack=stack, name="wbf") for _ in range(nbuf)]
        self.i = 0
        self.nbuf = nbuf

    def unit(self, w, r0, nk, c0, ncol=128):
        k = self.k
        s = self.stg[self.i % 2]
        b = self.wb[self.i % self.nbuf]
        ce = ("scalar", "vector")[self.i % 2]
        self.i += 1
        src = TV(w.base[r0 * 128:(r0 + nk) * 128, c0:c0 + ncol].rearrange("(c p) n -> p c n", p=128), w.buf)
        k.dma(s[:, 0:nk, 0:ncol], src, eng="sync")
        k.copy(ce, b[:, 0:nk, 0:ncol], s[:, 0:nk, 0:ncol])
        return b


def layer_norm_fm(k, S1, nch, ntok, onesD, psM, psV, mean_sb, sqt, rstd, g, b, after=None):
    for m in range(nch):
        k.mm(psM[:, 0:ntok], onesD[:, :], S1[:, m, :], start=(m == 0), stop=(m == nch - 1))
    k.copy("scalar", mean_sb[:, 0:ntok], psM[:, 0:ntok])
    for m in range(nch):
        k.tt(("vector", "gpsimd")[m % 2], S1[:, m, :], S1[:, m, :], mean_sb[:, 0:ntok], ALU.subtract)
    for m in range(nch):
        q = sqt[m % 2]
        k.act(q[:, 0:ntok], S1[:, m, :], AF.Square)
        k.mm(psV[:, 0:ntok], onesD[:, :], q[:, 0:ntok], start=(m == 0), stop=(m == nch - 1))
    k.act(rstd[:, 0:ntok], psV[:, 0:ntok], AF.Sqrt, bias=k.eps_ln[:, :])
    k.recip(rstd[:, 0:ntok], rstd[:, 0:ntok])
    for m in range(nch):
        k.tt("vector", S1[:, m, :], S1[:, m, :], rstd[:, 0:ntok], ALU.mult)
        k.act(S1[:, m, :], S1[:, m, :], AF.Identity, bias=b[:, m:m + 1], scale=g[:, m:m + 1])
        if after is not None:
            after(m)


def emit_phase3(k, l, xT, GA, GB, sel, wga, wgb, modT, lnv, proj_a, proj_b, w_out, wg, wu, wd, x_out, consts):
    st = ExitStack()
    TT = 512
    mod = k.sb([128, 96], stack=st)
    op1 = k.sb([128, 96], stack=st)
    lv = k.sb([128, 4, KC], stack=st)
    onesD = k.sb([128, 128], stack=st)
    k.eps_ln = k.sb([128, 1], stack=st)
    k.dma(mod[:, :], modT[:, :], eng="sync")
    k.dma(lv[:, :, :], lnv[:, :, :], eng="sync")
    selt = k.sb([128, 2], stack=st)
    k.dma(selt[:, :], sel[:, :], eng="sync")
    k.ts("vector", op1[:, :], mod[:, :], 1.0, None, ALU.add)
    k.memset("gpsimd", onesD[:, :], 1.0 / D)
    k.memset("gpsimd", k.eps_ln[:, :], LN_EPS)
    S1 = k.sb([128, KC, TT], stack=st, name="S1")
    Y1 = k.sb([128, KC, TT], BF16, stack=st, name="Y1")
    Y2 = k.sb([128, KC, TT], BF16, stack=st, name="Y2")
    MG = k.sb([128, KC, TT], BF16, stack=st, name="MG")
    Y3 = k.sb([128, KC, TT], BF16, stack=st, name="Y3")
    HT = k.sb([128, HC, TT], BF16, stack=st, name="HT")
    tmp = [k.sb([128, TT], stack=st, name="tmp") for _ in range(4)]
    gin = [k.sb([128, TT], stack=st, name="gin") for _ in range(4)]
    mean_sb = k.sb([128, TT], stack=st)
    rstd = k.sb([128, TT], stack=st)
    ws = WStream(k, st, nbuf=3)
    banks = [k.ps(stack=st) for _ in range(6)]
    psM = k.ps(stack=st)
    psV = k.ps(stack=st)
    bi = [0]

    def bank():
        b = banks[bi[0] % 6]
        bi[0] += 1
        return b

    G_M, SH_F, SC_F, G_F = 2 * KC, 3 * KC, 4 * KC, 5 * KC
    for t in range(NT // TT):
        tsl = slice(t * TT, (t + 1) * TT)
        for half in range(2):
            k.dma(S1[:, half * 8:(half + 1) * 8, :],
                  TV(xT.base[half * 8:(half + 1) * 8, :, tsl].rearrange("c p t -> p c t"), xT.buf))
        for kc in range(KC):
            k.ts(("vector", "gpsimd")[kc % 2], Y3[:, kc, :], S1[:, kc, :], op1[:, KC + kc:KC + kc + 1], mod[:, kc:kc + 1], ALU.mult, ALU.add)
        for src, dst in ((GA, Y1), (GB, Y2)):
            for fh in range(2):
                for th in range(2):
                    k.dma(S1[:, th * 8:(th + 1) * 8, :],
                          TV(src.base[fh, :, :, th * NT + t * TT:th * NT + (t + 1) * TT].rearrange("c p t -> p c t"), src.buf))
                k.ts("vector", S1[:, 0:8, :], S1[:, 0:8, :], selt[:, 0:1], None, ALU.mult)
                k.stt(dst[:, fh * 8:(fh + 1) * 8, :], S1[:, 8:16, :], selt[:, 1:2], S1[:, 0:8, :], ALU.mult, ALU.add)
        for m in range(KC):
            ga, gb = gin[(2 * m) % 4], gin[(2 * m + 1) % 4]
            wga_u = ws.unit(wga, 0, KC, m * 128)
            wgb_u = ws.unit(wgb, 0, KC, m * 128)
            pga, pgb = bank(), bank()
            for kc in range(KC):
                k.mm(pga[:, :], wga_u[:, kc, :], Y3[:, kc, :], start=(kc == 0), stop=(kc == KC - 1))
            for kc in range(KC):
                k.mm(pgb[:, :], wgb_u[:, kc, :], Y3[:, kc, :], start=(kc == 0), stop=(kc == KC - 1))
            k.act(ga[:, :], pga[:, :], AF.Sigmoid)
            k.act(gb[:, :], pgb[:, :], AF.Sigmoid)
            wa = ws.unit(proj_a, 0, KC, m * 128)
            wb_ = ws.unit(proj_b, 0, KC, m * 128)
            pa, pb = bank(), bank()
            for kc in range(KC):
                k.mm(pa[:, :], wa[:, kc, :], Y1[:, kc, :], start=(kc == 0), stop=(kc == KC - 1))
            for kc in range(KC):
                k.mm(pb[:, :], wb_[:, kc, :], Y2[:, kc, :], start=(kc == 0), stop=(kc == KC - 1))
            t1, t2 = tmp[(2 * m) % 4], tmp[(2 * m + 1) % 4]
            k.tt("vector", t1[:, :], pa[:, :], ga[:, :], ALU.mult)
            k.tt("vector", t2[:, :], pb[:, :], gb[:, :], ALU.mult)
            k.tt("gpsimd", MG[:, m, :], t1[:, :], t2[:, :], ALU.add)
        for m in range(KC):
            wo = ws.unit(w_out, 0, KC, m * 128)
            pw = bank()
            xi = gin[m % 4]
            k.dma(xi[:, :], TV(xT.base[m, :, tsl], xT.buf))
            for kc in range(KC):
                k.mm(pw[:, :], wo[:, kc, :], MG[:, kc, :], start=(kc == 0), stop=(kc == KC - 1))
            xa = tmp[m % 4]
            k.act(xa[:, :], xi[:, :], AF.Copy, scale=float(ALPHA))
            k.stt(S1[:, m, :], pw[:, :], mod[:, G_M + m:G_M + m + 1], xa[:, :], ALU.mult, ALU.add)

        def mk_u2(m):
            k.ts("vector", Y1[:, m, :], S1[:, m, :], op1[:, SC_F + m:SC_F + m + 1], mod[:, SH_F + m:SH_F + m + 1], ALU.mult, ALU.add)

        layer_norm_fm(k, S1, KC, TT, onesD, psM, psV, mean_sb, tmp[0:2], rstd, TL(lv.base[:, 0, :], lv.buf), TL(lv.base[:, 1, :], lv.buf), after=mk_u2)
        for j in range(HC):
            wgu = ws.unit(wg, 0, KC, j * 128)
            wuu = ws.unit(wu, 0, KC, j * 128)
            pg, pu = bank(), bank()
            for kc in range(KC):
                k.mm(pg[:, :], wgu[:, kc, :], Y1[:, kc, :], start=(kc == 0), stop=(kc == KC - 1))
            for kc in range(KC):
                k.mm(pu[:, :], wuu[:, kc, :], Y1[:, kc, :], start=(kc == 0), stop=(kc == KC - 1))
            sg = tmp[j % 4]
            k.act(sg[:, :], pg[:, :], AF.Silu)
            k.tt("vector", HT[:, j, :], sg[:, :], pu[:, :], ALU.mult)
        for m in range(KC):
            w1 = ws.unit(wd, 0, 22, m * 128)
            w2 = ws.unit(wd, 22, 22, m * 128)
            pd = bank()
            for j in range(HC):
                w = w1 if j < 22 else w2
                k.mm(pd[:, :], w[:, j % 22, :], HT[:, j, :], start=(j == 0), stop=(j == HC - 1))
            xa = tmp[m % 4]
            k.act(xa[:, :], S1[:, m, :], AF.Copy, scale=float(ALPHA))
            k.stt(S1[:, m, :], pd[:, :], mod[:, G_F + m:G_F + m + 1], xa[:, :], ALU.mult, ALU.add)
        layer_norm_fm(k, S1, KC, TT, onesD, psM, psV, mean_sb, tmp[0:2], rstd, TL(lv.base[:, 2, :], lv.buf), TL(lv.base[:, 3, :], lv.buf))
        for half in range(2):
            k.dma(TV(x_out.base[half * 8:(half + 1) * 8, :, tsl].rearrange("c p t -> p c t"), x_out.buf),
                  S1[:, half * 8:(half + 1) * 8, :])
    k.barrier()
    st.close()


def _fm(v, nch):
    return np.ascontiguousarray(np.asarray(v, np.float32).reshape(nch, 128).T)


def build_B():
    nc = bass.Bass("TRN2", target_bir_lowering=False)
    with ExitStack() as st:
        k = K(nc, st)
        IN = lambda n, s: k.dram(n, s, kind="ExternalInput")
        xT, yaT, ybT = [IN(n, [KC, 128, NT]) for n in ("xT", "yaT", "ybT")]
        modT = IN("modT", [128, 96])
        lnv = IN("lnv", [128, 4, KC])
        proj_a, proj_b, w_out, wga, wgb = [IN(n, [D, D]) for n in ("proj_a", "proj_b", "w_out", "wga", "wgb")]
        wg, wu = IN("wg", [D, FH]), IN("wu", [D, FH])
        wd = IN("wd", [FH, D])
        x_out = k.dram("x_out", [KC, 128, NT], kind="ExternalOutput")
        emit_phase3(k, 0, xT, yaT, ybT, wga, wgb, modT, lnv, proj_a, proj_b, w_out, wg, wu, wd, x_out, None)
        print('B sbuf remaining', nc.sbuf_bytes_remaining)
        k.p.finish()
        k.p.emit()
    return nc


def build_M():
    nc = bass.Bass("TRN2", target_bir_lowering=False)
    with ExitStack() as st:
        k = K(nc, st)
        cT = k.dram("cT", [128, KC], kind="ExternalInput")
        ada_w = k.dram("ada_w", [2, D, 6 * D], kind="ExternalInput")
        ada_bT = k.dram("ada_bT", [2, 128, 96], kind="ExternalInput")
        mod_out = k.dram("mod", [2, 128, 96], kind="ExternalOutput")
        emit_mod(k, cT, ada_w, ada_bT, mod_out)
        k.p.finish()
        k.p.emit()
    return nc


import math
G = 4
CH = 64
TG = G * CH
SH_M, SC_M = 0, KC
C_ID, C_ROT, C_B1, C_B64, C_NSU, C_NSL, C_SU, C_SL, C_IU, C_IL = range(10)
NCM = 10
V_MP, V_MN, V_PF = 0, 30, 60
(P_KK, P_KA, P_RK, P_LNW, P_LNB, P_W0F, P_W0B, P_A0F, P_A0B, P_V0) = range(10)
NV = 60 + 80


def host_consts():
    cm = np.zeros((128, NCM, 128), np.float32)
    p = np.arange(128)
    cm[p, C_ID, p] = 1.0
    for m in range(128):
        if m < 64:
            cm[m + 64, C_ROT, m] = -1.0
        else:
            cm[m - 64, C_ROT, m] = 1.0
    blk = (p[:, None] // 64) == (p[None, :] // 64)
    cm[:, C_B1, :] = blk
    cm[:, C_B64, :] = blk / 64.0
    j = p[:, None] % 64
    t = p[None, :] % 64
    cm[:, C_NSU, :] = -1.0 * (blk & (j < t))
    cm[:, C_NSL, :] = -1.0 * (blk & (j > t))
    cm[:, C_SU, :] = (blk & (j < t))
    cm[:, C_SL, :] = (blk & (j > t))
    cm[:, C_IU, 0:64] = (j <= np.arange(64)[None, :])
    cm[:, C_IL, 0:64] = (j >= np.arange(64)[None, :])
    m01 = np.ones((128, TG), np.float32)
    m01[:, ::CH] = 0.0
    pos = np.arange(T, dtype=np.float32)
    inv = (np.float32(10000.0) ** (-np.arange(0, 128, 2, dtype=np.float32) / np.float32(128))).astype(np.float32)
    ang = pos[:, None] * inv[None, :]
    emb = np.concatenate([ang, ang], -1)
    cosT = np.ascontiguousarray(np.cos(emb).astype(np.float32).T)
    sinT = np.ascontiguousarray(np.sin(emb).astype(np.float32).T)
    return cm, m01, cosT, sinT


def emit_phaseA(k, l, xsrc, modT, Wc, vec, lwd, lwi, lwg, lwv, lamv, subw, cosT, sinT, cmat, m01d, vf_in, yaT, ybT, vf_out, Zs, ZsB):
    NRW = 54 + (1 if l == 1 else 0)
    lam_init = 0.8 - 0.6 * math.exp(-0.3 * l)
    GN_EPS = 1e-5 * 64

    def zrow(j, c0, c1):
        return TV(Zs.base[j, :, c0:c1], ZsB[j])

    st = ExitStack()
    mod = k.sb([128, 96], stack=st)
    op1 = k.sb([128, 96], stack=st)
    k.dma(mod[:, :], modT[:, :], eng="sync")
    k.ts("vector", op1[:, :], mod[:, :], 1.0, None, ALU.add)
    zt = k.sb([128, NRW, 1], stack=st)
    k.memset("gpsimd", zt[:, :, :], 0.0)
    for j in range(NRW):
        k.dma(zrow(j, 0, 1), zt[:, j, :], eng="sync", slow=True)
        k.dma(zrow(j, T + 1, T + 2), zt[:, j, :], eng="sync", slow=True)
    U = k.sb([128, KC, T], BF16, stack=st, name="U")
    zb = [k.sb([128, 2048], stack=st, name="zb") for _ in range(2)]
    for i in range(T // 128):
        xs = zb[i % 2]
        xv = TL(xs.base[:, :].rearrange("p (c t) -> p c t", c=KC), xs.buf)
        for dsl, srcv in xsrc(i):
            k.dma(xv[:, dsl, :], srcv)
        for kc in range(KC):
            k.ts(("vector", "gpsimd")[kc % 2], U[:, kc, i * 128:(i + 1) * 128], xv[:, kc, :],
                 op1[:, SC_M + kc:SC_M + kc + 1], mod[:, SH_M + kc:SH_M + kc + 1], ALU.mult, ALU.add)
    ws = WStream(k, st, kmax=KC, nbuf=3)
    banks = [k.ps(stack=st) for _ in range(8)]
    for j in range(NRW):
        w = ws.unit(Wc, 0, KC, j * 128)
        for th in range(2):
            for n in range(4):
                pb = banks[th * 4 + n]
                for kc in range(KC):
                    t0 = th * 2048 + n * 512
                    k.mm(pb[:, :], w[:, kc, :], U[:, kc, t0:t0 + 512], start=(kc == 0), stop=(kc == KC - 1))
            z = zb[th]
            for n in range(4):
                k.copy(("scalar", "vector")[n % 2], z[:, n * 512:(n + 1) * 512], banks[th * 4 + n][:, :])
            k.dma(zrow(j, 1 + th * 2048, 1 + (th + 1) * 2048), z[:, :])
    k.barrier()
    st.close()

    st = ExitStack()
    vecs = k.sb([128, NV], stack=st)
    k.dma(vecs[:, :], vec[:, :], eng="sync")
    c0 = k.sb([128, 30], stack=st)
    k.tt("vector", c0[:, :], vecs[:, V_MP:V_MP + 30], vecs[:, V_MN:V_MN + 30], ALU.add)
    k.ts("vector", c0[:, :], c0[:, :], -1.0, 1.0, ALU.mult, ALU.add)
    pf = lambda i, fc: vecs[:, V_PF + 8 * i + fc:V_PF + 8 * i + fc + 1]
    omka = k.sb([128, 8], stack=st)
    k.ts("vector", omka[:, :], vecs[:, V_PF + 8 * P_KA:V_PF + 8 * P_KA + 8], -1.0, 1.0, ALU.mult, ALU.add)
    k.omka2 = k.sb([128, 8], stack=st)
    k.ts("vector", k.omka2[:, :], vecs[:, V_PF + 8 * P_KA:V_PF + 8 * P_KA + 8], -2.0, 2.0, ALU.mult, ALU.add)
    k.eps_gn = k.sb([128, 1], stack=st)
    k.memset("gpsimd", k.eps_gn[:, :], GN_EPS)
    cm = k.sb([128, NCM, 128], stack=st)
    k.dma(cm[:, :, :], cmat[:, :, :], eng="sync")
    idb = k.sb([128, 128], BF16, stack=st)
    k.copy("vector", idb[:, :], cm[:, C_ID, :])
    m01 = k.sb([128, TG], stack=st)
    k.dma(m01[:, :], m01d[:, :], eng="sync")
    Rr, Kr, Vr, KKr = [k.sb([128, T], stack=st, name=n) for n in ("Rr", "Kr", "Vr", "KKr")]
    stg = Rr
    DEC = k.sb([128, 2, 8, 128], BF16, stack=st)
    ICL = k.sb([128, 2, 8, 128], BF16, stack=st)
    GAT = k.sb([128, 2, 8, 128], BF16, stack=st)
    VUP = k.sb([128, 8, 128], BF16, stack=st)
    for src, dst in ((lwd, DEC), (lwi, ICL), (lwg, GAT)):
        sv = TL(stg.base[:, 0:2048].rearrange("p (a b c) -> p a b c", a=2, b=8), stg.buf)
        k.dma(sv[:, :, :, :], src[:, :, :, :], eng="sync")
        k.copy("vector", dst[:, :, :, :], sv[:, :, :, :])
    if l == 1:
        sv = TL(stg.base[:, 0:1024].rearrange("p (b c) -> p b c", b=8), stg.buf)
        k.dma(sv[:, :, :], lwv[:, :, :], eng="sync")
        k.copy("vector", VUP[:, :, :], sv[:, :, :])
    LORd = TL(k.nc.dram_tensor("LORd%d" % l, [6, 128, T], BF16).ap(), k.p.buf())
    DN = k.sb([128, T], BF16, stack=st, name="DN") if l == 1 else None
    SW = 512
    PP = [k.sb([128, 512], stack=st, name="pp%d" % i) for i in range(6)]
    lorp = k.sb([128, 4, 512], BF16, stack=st)
    zin = [k.sb([128, SW + 2], stack=st, name="zin") for _ in range(2)]
    sht = PP[4:6]
    cnt = [0]

    def shift_tile(j, t0, n, out):
        i = cnt[0]
        cnt[0] += 1
        zi = zin[i % 2]
        t1 = sht[i % 2]
        k.dma(zi[:, 0:n + 2], zrow(j, t0, t0 + n + 2))
        k.act(t1[:, 0:n], zi[:, 1:n + 1], AF.Identity, scale=c0[:, j:j + 1])
        k.stt(t1[:, 0:n], zi[:, 0:n], vecs[:, V_MP + j:V_MP + j + 1], t1[:, 0:n], ALU.mult, ALU.add)
        k.stt(out, zi[:, 2:n + 2], vecs[:, V_MN + j:V_MN + j + 1], t1[:, 0:n], ALU.mult, ALU.add)

    lt = PP[0:2]
    lb = [TL(lorp.base[:, i, :], lorp.buf) for i in range(2)]
    for r in range(6):
        for t in range(T // SW):
            o = lt[(r * 8 + t) % 2]
            ob = lb[(r * 8 + t) % 2]
            shift_tile(24 + r, t * SW, SW, o[:, :])
            fn = (AF.Tanh, AF.Tanh, AF.Identity, AF.Identity, AF.Sigmoid, AF.Sigmoid)[r]
            k.act(ob[:, :], o[:, :], fn)
            k.dma(TV(LORd.base[r, :, t * SW:(t + 1) * SW], LORd.buf), ob[:, :])
    if l == 1:
        for t in range(T // SW):
            zi = zin[t % 2]
            k.dma(zi[:, 0:SW], zrow(54, 1 + t * SW, 1 + (t + 1) * SW))
            k.copy("vector", DN[:, t * SW:(t + 1) * SW], zi[:, 0:SW])

    import types
    import itertools
    OD = [[TL(k.nc.dram_tensor("OD%d_%d_%d" % (l, pp_, d), [128, T], F32).ap(), k.p.buf()) for d in range(2)] for pp_ in range(2)]
    BON = [TL(k.nc.dram_tensor("BON%d_%d" % (l, pp_), [128, T], F32).ap(), k.p.buf()) for pp_ in range(2)]

    def mk_stream(si):
        R = types.SimpleNamespace()
        R.A, R.B, R.C = [k.ps([128, G, 128], stack=st) for _ in range(3)]
        Mh = st.enter_context(k.nc.psum_tensor(k.name("psM"), [128, 512], F32))
        R.Mlo = TL(Mh[:, 0:256], k.p.buf())
        R.Mhi = TL(Mh[:, 256:512], R.Mlo.buf)
        R.Cf = TL(R.C.base[:, :, :].rearrange("p c t -> p (c t)"), R.C.buf)
        f32t = lambda name: k.sb([128, TG], stack=st, name=name)
        (R.tA, R.tLW, R.tL, R.tPm, R.tSx, R.eKt, R.eNg, R.eR, R.eH, R.tKd, R.tB, R.tRt) = [f32t("f%d_%d" % (si, i)) for i in range(12)]
        R.tSf = f32t("tSf") if si == 1 else None
        R.gC = k.sb([128, G], stack=st)
        R.Dt = {n: k.sb([128, G, 128], BF16, stack=st, name=n) for n in ("KtD", "BtD", "KkD", "BhD", "KhD", "VD")}
        for n in R.Dt:
            k.memset("gpsimd", R.Dt[n][:, :, :], 0.0)
        R.RtS = k.sb([128, TG], BF16, stack=st)
        bt = lambda name: k.sb([128, G, 128], BF16, stack=st, name=name)
        R.Xs = [bt("X0"), bt("X1")]
        R.XTs = [bt("XT0"), bt("XT1")]
        R.Tms = [bt("T0"), bt("T1")]
        R.MkT, R.KtT, R.VT, R.BhT, R.KhT, R.WT, R.MkTt, R.U0T = [bt(n) for n in ("MkT", "KtT", "VT", "BhT", "KhT", "WT", "MkTt", "U0T")]
        R.NbS = k.sb([128, G, 64], BF16, stack=st)
        R.NkS = k.sb([128, G, 64], BF16, stack=st)
        R.P32 = k.sb([128, G, 128], stack=st)
        R.QT32 = k.sb([128, G, 128], stack=st)
        R.DG = k.sb([128, G, 128], stack=st)
        R.Rp32 = k.sb([128, G, 64], stack=st)
        R.Op32 = k.sb([128, G, 64], stack=st)
        R.STs = [k.sb([128, 128], stack=st, name="ST") for _ in range(2)]
        R.OT = [k.sb([128, TG], stack=st, name="OT")] * 2
        R.lor = [k.sb([128, 2, TG], BF16, stack=st, name="lor") for _ in range(2)]
        R.e1, R.e2 = ("vector", "gpsimd") if si == 0 else ("gpsimd", "vector")
        return R

    RS = [mk_stream(0), mk_stream(1)]
    Rr, Kr, Vr, KKr = Rr, Kr, Vr, KKr
    bc = lambda slot, w=128: TV(cm.base[:, slot:slot + 1, 0:w].to_broadcast([128, G, w]), cm.buf)
    g3 = lambda tl: TV(tl.base[:, :].rearrange("p (c t) -> p c t", c=G), tl.buf)

    def blkD(eng, dst, src_a, src_b, op=ALU.mult):
        for hh in range(2):
            ps_ = slice(hh * 64, hh * 64 + 64)
            o = TV(dst.base[ps_, :, hh * 64:hh * 64 + 64], dst.buf)
            a = TV(src_a.ap[ps_, :].rearrange("p (c t) -> p c t", c=G), src_a.buf)
            if src_b is None:
                k.copy(eng, o, a)
            else:
                b = TV(src_b.ap[ps_, :].rearrange("p (c t) -> p c t", c=G), src_b.buf)
                k.tt(eng, o, a, b, op)

    def sweep_gen(fc, d, R):
        e1, e2 = R.e1, R.e2
        k.memset("gpsimd", R.STs[0][:, :], 0.0)
        sti = 0
        NG = T // TG
        order = range(NG) if d == 0 else range(NG - 1, -1, -1)
        nM, nMT, mI = (C_NSU, C_NSL, C_IU) if d == 0 else (C_NSL, C_NSU, C_IL)
        mSL = C_SL if d == 0 else C_SU
        KtD, BtD, KkD, BhD, KhD, VD = [R.Dt[n] for n in ("KtD", "BtD", "KkD", "BhD", "KhD", "VD")]
        tA, tLW, tL, tPm, tSx, tSf, eKt, eNg, eR, eH, tKd, tB, tRt = (R.tA, R.tLW, R.tL, R.tPm, R.tSx, R.tSf, R.eKt, R.eNg, R.eR, R.eH, R.tKd, R.tB, R.tRt)
        assert d == 0 or tSf is not None
        it = 0
        olist = list(order)

        def load_lor(j):
            lj = R.lor[j % 2]
            ts_ = slice(olist[j] * TG, (olist[j] + 1) * TG)
            k.dma(lj[:, 0, :], TV(LORd.base[d, :, ts_], LORd.buf))
            k.dma(lj[:, 1, :], TV(LORd.base[2 + d, :, ts_], LORd.buf))

        load_lor(0)
        for n in order:
            tsl = slice(n * TG, (n + 1) * TG)
            lor = R.lor[it % 2]
            OT = R.OT[it % 2]
            it += 1
            k.mm(R.Mlo[:, :], ICL[:, d, fc, :], lor[:, 1, :])
            k.mm(R.Mhi[:, :], DEC[:, d, fc, :], lor[:, 0, :])
            k.act(tA[:, :], R.Mlo[:, :], AF.Sigmoid, bias=pf(P_A0F + d, fc))
            k.act(tLW[:, :], R.Mhi[:, :], AF.Sigmoid, bias=pf(P_W0F + d, fc))
            if it < len(olist):
                load_lor(it)
            yield
            k.ts(e2, tLW[:, :], tLW[:, :], -math.exp(-0.5), None, ALU.mult)
            k.scan(tL[:, :], m01[:, :], tLW[:, :], 0.0, ALU.mult, ALU.add)
            L3 = g3(tL)
            Ltot = TV(tL.base[:, :].rearrange("p (c t) -> p c t", c=G)[:, :, CH - 1:CH], tL.buf)
            gC3 = TV(R.gC.base[:, :].rearrange("p (c o) -> p c o", o=1), R.gC.buf)
            k.act(gC3, Ltot, AF.Exp)
            k.tt(e2, tPm[:, :], tL[:, :], tLW[:, :], ALU.subtract)
            k.tt(e1, g3(tSx), TV(Ltot.ap.to_broadcast([128, G, CH]), tL.buf), L3, ALU.subtract)
            if d == 0:
                k.act(eKt[:, :], tPm[:, :], AF.Exp)
                k.act(eNg[:, :], tL[:, :], AF.Exp, scale=-1.0)
                k.act(eR[:, :], tL[:, :], AF.Exp)
                k.act(eH[:, :], tSx[:, :], AF.Exp)
            else:
                k.tt(e2, tSf[:, :], tSx[:, :], tLW[:, :], ALU.add)
                k.act(eKt[:, :], tSx[:, :], AF.Exp)
                k.act(eNg[:, :], tSf[:, :], AF.Exp, scale=-1.0)
                k.act(eR[:, :], tSf[:, :], AF.Exp)
                k.act(eH[:, :], tPm[:, :], AF.Exp)
            k.ts(e1, tKd[:, :], tA[:, :], pf(P_KA, fc), omka[:, fc:fc + 1], ALU.mult, ALU.add)
            k.tt(e2, tKd[:, :], tKd[:, :], Kr[:, tsl], ALU.mult)
            k.tt(e2, tB[:, :], KKr[:, tsl], tA[:, :], ALU.mult)
            yield
            blkD(e1, KtD, KKr[:, tsl], eKt[:, :])
            blkD(e2, BtD, tB[:, :], eNg[:, :])
            blkD(e1, KkD, tKd[:, :], eNg[:, :])
            blkD(e2, BhD, tB[:, :], eH[:, :])
            blkD(e1, KhD, tKd[:, :], eH[:, :])
            blkD(e2, VD, Vr[:, tsl], None)
            k.tt(e1, tRt[:, :], Rr[:, tsl], eR[:, :], ALU.mult)
            k.copy(e2, R.RtS[:, :], tRt[:, :])
            yield
            RtS = R.RtS
            for c in range(G):
                k.mm(R.A[:, c, :], BtD[:, c, :], KtD[:, c, :])
            for c in range(G):
                k.mm(R.Mlo[:, c * 64:(c + 1) * 64], BtD[:, c, :], RtS[:, c * 64:(c + 1) * 64])
            for c in range(G):
                k.mm(R.B[:, c, :], KtD[:, c, :], BtD[:, c, :])
            for c in range(G):
                k.mm(R.C[:, c, :], KtD[:, c, :], KkD[:, c, :])
            for c in range(G):
                k.mm(R.Mhi[:, c * 64:(c + 1) * 64], KkD[:, c, :], RtS[:, c * 64:(c + 1) * 64])
            X, XT, Tm = R.Xs[0], R.XTs[0], R.Tms[0]
            k.tt("vector", X[:, :, :], R.A[:, :, :], bc(nM), ALU.mult)
            k.tt("vector", XT[:, :, :], R.B[:, :, :], bc(nMT), ALU.mult)
            k.tt("vector", R.MkT[:, :, :], R.C[:, :, :], bc(mSL), ALU.mult)
            k.tt("vector", R.NbS[:, :, :], g3(R.Mlo), bc(mI, 64), ALU.mult)
            k.tt("vector", R.NkS[:, :, :], g3(R.Mhi), bc(mI, 64), ALU.mult)
            k.tt("gpsimd", Tm[:, :, :], X[:, :, :], bc(C_ID), ALU.add)
            yield
            for i in range(1, 6):
                Xn, XTn, Tn = R.Xs[i % 2], R.XTs[i % 2], R.Tms[i % 2]
                if i < 5:
                    for c in range(G):
                        k.mm(R.A[:, c, :], XT[:, c, :], X[:, c, :])
                for c in range(G):
                    k.mm(R.B[:, c, :], X[:, c, :], XT[:, c, :])
                if i < 5:
                    k.copy("scalar", Xn[:, :, :], R.A[:, :, :])
                k.copy("vector", XTn[:, :, :], R.B[:, :, :])
                yield
                for c in range(G):
                    k.mm(R.C[:, c, :], XTn[:, c, :], Tm[:, c, :], start=True, stop=False)
                    k.mm(R.C[:, c, :], idb[:, :], Tm[:, c, :], start=False, stop=True)
                k.copy("scalar", Tn[:, :, :], R.C[:, :, :])
                X, XT, Tm = Xn, XTn, Tn
                yield
            for src, dst, ce, pb_ in ((KtD, R.KtT, "vector", R.A), (VD, R.VT, "scalar", R.B), (BhD, R.BhT, "vector", R.C), (KhD, R.KhT, "scalar", R.A)):
                for c in range(G):
                    k.mm(pb_[:, c, :], src[:, c, :], idb[:, :])
                k.copy(ce, dst[:, :, :], pb_[:, :, :])
            yield
            for c in range(G):
                k.mm(R.A[:, c, :], Tm[:, c, :], R.KtT[:, c, :])
            for c in range(G):
                k.mm(R.B[:, c, :], R.MkT[:, c, :], Tm[:, c, :])
            k.act(R.WT[:, :, :], R.A[:, :, :], AF.Copy, scale=-1.0)
            k.copy("vector", R.MkTt[:, :, :], R.B[:, :, :])
            yield
            for c in range(G):
                k.mm(R.C[:, c, :], R.MkTt[:, c, :], R.VT[:, c, :])
            for c in range(G):
                k.mm(R.A[:, c, :], R.WT[:, c, :], R.BhT[:, c, :])
            for c in range(G):
                k.mm(R.Mlo[:, c * 64:(c + 1) * 64], R.WT[:, c, :], R.NbS[:, c, :])
            k.act(R.U0T[:, :, :], R.C[:, :, :], AF.Copy, scale=-1.0)
            gCb = TV(R.gC.base[:, :].rearrange("p (c o) -> p c o", o=1).to_broadcast([128, G, 128]), R.gC.buf)
            k.tt("gpsimd", R.DG[:, :, :], bc(C_ID), gCb, ALU.mult)
            k.tt("vector", R.P32[:, :, :], R.A[:, :, :], R.DG[:, :, :], ALU.add)
            k.tt("vector", R.Rp32[:, :, :], g3(R.Mlo), g3(tRt), ALU.add)
            yield
            for c in range(G):
                k.mm(R.B[:, c, :], R.BhT[:, c, :], R.U0T[:, c, :], start=True, stop=False)
                k.mm(R.B[:, c, :], R.KhT[:, c, :], R.VT[:, c, :], start=False, stop=True)
            for c in range(G):
                k.mm(R.Mhi[:, c * 64:(c + 1) * 64], R.U0T[:, c, :], R.NbS[:, c, :], start=True, stop=False)
                k.mm(R.Mhi[:, c * 64:(c + 1) * 64], R.VT[:, c, :], R.NkS[:, c, :], start=False, stop=True)
            k.copy("scalar", R.QT32[:, :, :], R.B[:, :, :])
            k.copy("scalar", R.Op32[:, :, :], g3(R.Mhi))
            yield
            corder = range(G) if d == 0 else range(G - 1, -1, -1)
            for c in corder:
                ST = R.STs[sti % 2]
                STn = R.STs[(sti + 1) % 2]
                sti += 1
                k.mm(R.Cf[:, c * 64:(c + 1) * 64], ST[:, :], R.Rp32[:, c, :])
                k.mm(R.Cf[:, 256:384], R.P32[:, c, :], ST[:, :])
                k.tt("vector", STn[:, :], R.Cf[:, 256:384], R.QT32[:, c, :], ALU.add)
                yield
            k.tt("vector", OT[:, :], R.Cf[:, 0:TG], TV(R.Op32.base[:, :, :].rearrange("p c t -> p (c t)"), R.Op32.buf), ALU.add)
            k.dma(TV(OD[fc % 2][d].base[:, tsl], OD[fc % 2][d].buf), OT[:, :])
            yield

    def post_gen(fc):
        for n in range(T // 512):
            tsl = slice(n * 512, (n + 1) * 512)
            R = RS[n % 2]
            pO, pF, pX, pY, pA2, pW = PP
            k.dma(pO[:, :], TV(OD[fc % 2][0].base[:, tsl], OD[fc % 2][0].buf))
            k.dma(pF[:, :], TV(OD[fc % 2][1].base[:, tsl], OD[fc % 2][1].buf))
            k.dma(pX[:, :], TV(BON[fc % 2].base[:, tsl], BON[fc % 2].buf))
            k.dma(lorp[:, 2:4, :], TV(LORd.base[4:6, :, tsl].rearrange("r p t -> p r t"), LORd.buf))
            k.tt("gpsimd", pO[:, :], pO[:, :], pF[:, :], ALU.add)
            yield
            for hh in range(2):
                hs = slice(hh * 256, (hh + 1) * 256)
                k.mm((R.Mlo, R.Mhi)[hh][:, :], cm[:, C_B64, :], pO[:, hs])
                k.tt("vector", pO[:, hs], pO[:, hs], (R.Mlo, R.Mhi)[hh][:, :], ALU.subtract)
            k.act(pY[:, :], pO[:, :], AF.Square)
            yield
            for hh in range(2):
                hs = slice(hh * 256, (hh + 1) * 256)
                k.mm((R.Mlo, R.Mhi)[hh][:, :], cm[:, C_B64, :], pY[:, hs])
                k.act(pW[:, hs], (R.Mlo, R.Mhi)[hh][:, :], AF.Sqrt, bias=k.eps_gn[:, :])
            k.recip(pW[:, :], pW[:, :])
            k.tt("vector", pO[:, :], pO[:, :], pW[:, :], ALU.mult)
            k.act(pO[:, :], pO[:, :], AF.Identity, bias=pf(P_LNB, fc), scale=pf(P_LNW, fc))
            k.tt("gpsimd", pO[:, :], pO[:, :], pX[:, :], ALU.add)
            yield
            for hh in range(2):
                hs = slice(hh * 256, (hh + 1) * 256)
                M_ = (R.Mlo, R.Mhi)[hh]
                k.mm(M_[:, :], GAT[:, 0, fc, :], lorp[:, 2, hs], start=True, stop=False)
                k.mm(M_[:, :], GAT[:, 1, fc, :], lorp[:, 3, hs], start=False, stop=True)
                k.tt("vector", pF[:, hs], pO[:, hs], M_[:, :], ALU.mult)
            k.dma(TV(yaT.base[fc, :, tsl], yaT.buf), pF[:, :])
            yield

    prev_post = None
    for fc in range(8):
        R0 = RS[0]
        for t in range(T // SW):
            tsl = slice(t * SW, (t + 1) * SW)
            shift_tile(fc, t * SW, SW, Rr[:, tsl])
            shift_tile(8 + fc, t * SW, SW, Kr[:, tsl])
            shift_tile(16 + fc, t * SW, SW, Vr[:, tsl])
        for n in range(T // 512):
            tsl = slice(n * 512, (n + 1) * 512)
            pX, pY, pB, pK = PP[0], PP[1], PP[2], PP[3]
            if l == 1:
                for hh in range(2):
                    hs = slice(n * 512 + hh * 256, n * 512 + (hh + 1) * 256)
                    k.mm((R0.Mlo, R0.Mhi)[hh][:, :], VUP[:, fc, :], DN[:, hs])
                    k.act(pX[:, hh * 256:(hh + 1) * 256], (R0.Mlo, R0.Mhi)[hh][:, :], AF.Sigmoid, bias=pf(P_V0, fc))
                k.dma(pY[:, :], TV(vf_in.base[fc, :, tsl], vf_in.buf))
                k.tt("gpsimd", pY[:, :], pY[:, :], Vr[:, tsl], ALU.subtract)
                k.tt("vector", pY[:, :], pY[:, :], pX[:, :], ALU.mult)
                k.tt("gpsimd", Vr[:, tsl], Vr[:, tsl], pY[:, :], ALU.add)
            k.ts("vector", KKr[:, tsl], Kr[:, tsl], pf(P_KK, fc), None, ALU.mult)
            k.act(pB[:, :], KKr[:, tsl], AF.Square)
            for hh in range(2):
                k.mm((R0.Mlo, R0.Mhi)[hh][:, :], cm[:, C_B1, :], pB[:, hh * 256:(hh + 1) * 256])
                k.ts("vector", pK[:, hh * 256:(hh + 1) * 256], (R0.Mlo, R0.Mhi)[hh][:, :], 1e-24, None, ALU.max)
            k.act(pK[:, :], pK[:, :], AF.Sqrt)
            k.recip(pK[:, :], pK[:, :])
            k.tt("gpsimd", KKr[:, tsl], KKr[:, tsl], pK[:, :], ALU.mult)
            qA, qB = PP[4], PP[5]
            k.dma(lorp[:, 0:2, :], TV(LORd.base[2:4, :, tsl].rearrange("r p t -> p r t"), LORd.buf))
            for hh in range(2):
                hs = slice(hh * 256, (hh + 1) * 256)
                k.mm(R0.Mlo[:, :], ICL[:, 0, fc, :], lorp[:, 0, hs])
                k.mm(R0.Mhi[:, :], ICL[:, 1, fc, :], lorp[:, 1, hs])
                k.act(qA[:, hs], R0.Mlo[:, :], AF.Sigmoid, bias=pf(P_A0F, fc))
                k.act(qB[:, hs], R0.Mhi[:, :], AF.Sigmoid, bias=pf(P_A0B, fc))
            k.tt("gpsimd", qA[:, :], qA[:, :], qB[:, :], ALU.add)
            k.ts("vector", qA[:, :], qA[:, :], pf(P_KA, fc), k.omka2[:, fc:fc + 1], ALU.mult, ALU.add)
            k.tt("gpsimd", qA[:, :], qA[:, :], Kr[:, tsl], ALU.mult)
            k.stt(qA[:, :], qA[:, :], pf(P_RK, fc), Rr[:, tsl], ALU.mult, ALU.mult)
            for hh in range(2):
                hs = slice(hh * 256, (hh + 1) * 256)
                k.mm((R0.Mlo, R0.Mhi)[hh][:, :], cm[:, C_B1, :], qA[:, hs])
                k.tt("vector", qB[:, hs], (R0.Mlo, R0.Mhi)[hh][:, :], Vr[:, n * 512 + hh * 256:n * 512 + (hh + 1) * 256], ALU.mult)
            k.dma(TV(BON[fc % 2].base[:, tsl], BON[fc % 2].buf), qB[:, :])
        if l == 0:
            for t in range(4):
                tsl = slice(t * 1024, (t + 1) * 1024)
                k.dma(TV(vf_out.base[fc, :, tsl], vf_out.buf), Vr[:, tsl])
        gens = [sweep_gen(fc, 0, RS[0]), sweep_gen(fc, 1, RS[1])]
        if prev_post is not None:
            gens.append(prev_post)
        for _ in itertools.zip_longest(*gens):
            pass
        prev_post = post_gen(fc)
    for _ in prev_post:
        pass
    k.barrier()
    st.close()

    st = ExitStack()
    cm = k.sb([128, NCM, 128], stack=st)
    k.dma(cm[:, :, :], cmat[:, :, :], eng="sync")
    cosS = k.sb([128, T], stack=st)
    sinS = k.sb([128, T], stack=st)
    k.dma(cosS[:, :], cosT[:, :], eng="sync")
    k.dma(sinS[:, :], sinT[:, :], eng="gpsimd")
    lv = k.sb([128, 4, 128], stack=st)
    k.dma(lv[:, :, :], lamv[:, :, :], eng="sync")
    sw = k.sb([128, 256], stack=st)
    k.dma(sw[:, :], subw[:, :], eng="sync")
    k.ts("vector", sw[:, :], sw[:, :], float(1.0 - lam_init), None, ALU.mult)
    lt1 = k.sb([128, 2, 128], stack=st)
    ls = k.sb([128, 2], stack=st)
    k.tt("vector", lt1[:, 0, :], lv[:, 0, :], lv[:, 1, :], ALU.mult)
    k.tt("vector", lt1[:, 1, :], lv[:, 2, :], lv[:, 3, :], ALU.mult)
    k.p.op("vector", lambda e: e.tensor_reduce(out=ls[:, :].ap, in_=lt1[:, :, :].ap, axis=AX.X, op=ALU.add),
           reads=[lt1.buf], writes=[ls.buf])
    k.act(ls[:, :], ls[:, :], AF.Exp)
    nlam = k.sb([128, 1], stack=st)
    k.tt("vector", nlam[:, :], ls[:, 1:2], ls[:, 0:1], ALU.subtract)
    k.ts("vector", nlam[:, :], nlam[:, :], float(-lam_init), None, ALU.add)
    eps_s = k.sb([128, 1], stack=st)
    k.memset("gpsimd", eps_s[:, :], 1e-5)
    QK = [k.sb([128, T], BF16, stack=st, name="qk%d" % i) for i in range(4)]
    V1 = k.sb([128, T // 128, 260], BF16, stack=st, name="V1")
    k.memset("gpsimd", V1[:, :, 256:257], 1.0)
    zl = [k.sb([128, 512], stack=st, name="zl") for _ in range(2)]
    r1 = [k.sb([128, 512], stack=st, name="r1") for _ in range(2)]
    r2 = [k.sb([128, 512], stack=st, name="r2") for _ in range(2)]
    pT = [k.sb([128, 256], BF16, stack=st, name="pT") for _ in range(3)]
    Oacc = [[k.ps([128, 512], stack=st) for _ in range(2)] for _ in range(2)]
    SP = [k.ps([128, 512], stack=st) for _ in range(2)]
    PTf = k.ps([128, 4, 128], stack=st)
    PR = k.ps([128, 512], stack=st)
    osb = [k.sb([128, 256], stack=st, name="osb") for _ in range(2)]
    tsb = [k.sb([128, 256], stack=st, name="tsb") for _ in range(2)]
    sm = [k.sb([128, 8], stack=st, name="sm") for _ in range(2)]
    yrow = [k.sb([128, 2, 256], stack=st, name="yrow") for _ in range(2)]
    scale = 128 ** -0.5
    it = 0
    for hd in range(4):
        rows = (30 + 2 * hd, 31 + 2 * hd, 38 + 2 * hd, 39 + 2 * hd)
        for ri, j in enumerate(rows):
            for t in range(8):
                tsl = slice(t * 512, (t + 1) * 512)
                z = zl[it % 2]
                a, b_ = r1[it % 2], r2[it % 2]
                it += 1
                k.dma(z[:, :], zrow(j, 1 + t * 512, 1 + (t + 1) * 512))
                k.mm(PR[:, :], cm[:, C_ROT, :], z[:, :])
                k.tt("gpsimd", a[:, :], z[:, :], cosS[:, tsl], ALU.mult)
                k.tt("vector", b_[:, :], PR[:, :], sinS[:, tsl], ALU.mult)
                k.tt("gpsimd", QK[ri][:, tsl], a[:, :], b_[:, :], ALU.add)
        for ec in range(2):
            j = 46 + 2 * hd + ec
            for t in range(8):
                z = zl[it % 2]
                it += 1
                k.dma(z[:, :], zrow(j, 1 + t * 512, 1 + (t + 1) * 512))
                for q in range(4):
                    k.tr(PTf[:, q, :], z[:, q * 128:(q + 1) * 128], cm[:, C_ID, :])
                k.copy("vector", V1[:, t * 4:(t + 1) * 4, ec * 128:(ec + 1) * 128], PTf[:, :, :])
        for qt in range(T // 256):
            qsl = slice(qt * 256, (qt + 1) * 256)
            steps = [(kt, m) for kt in range(T // 128) for m in range(2)]
            NKT = T // 128

            def S_(i):
                kt, m = steps[i]
                k.mm(SP[i % 2][:, 0:256], QK[2 + m][:, kt * 128:(kt + 1) * 128], QK[m][:, qsl])

            def EPV_(i):
                kt, m = steps[i]
                p_ = pT[i % 3]
                k.act(p_[:, :], SP[i % 2][:, 0:256], AF.Exp, scale=float(scale))
                for qb in range(2):
                    k.mm(Oacc[m][qb][:, 0:257], p_[:, qb * 128:(qb + 1) * 128], V1[:, kt, 0:257],
                         start=(kt == 0), stop=(kt == NKT - 1))

            S_(0)
            for i in range(len(steps)):
                if i + 1 < len(steps):
                    S_(i + 1)
                EPV_(i)
            yr = yrow[qt % 2]
            for qb in range(2):
                s_ = sm[qb]
                o_, t_ = osb[qb], tsb[qb]
                k.recip(s_[:, 0:1], Oacc[0][qb][:, 256:257])
                k.recip(s_[:, 1:2], Oacc[1][qb][:, 256:257])
                k.tt("vector", s_[:, 2:3], s_[:, 1:2], nlam[:, :], ALU.mult)
                k.ts("vector", t_[:, :], Oacc[1][qb][:, 0:256], s_[:, 2:3], None, ALU.mult)
                k.stt(o_[:, :], Oacc[0][qb][:, 0:256], s_[:, 0:1], t_[:, :], ALU.mult, ALU.add)
                k.act(t_[:, :], o_[:, :], AF.Square, accum=s_[:, 3:4])
                k.act(s_[:, 4:5], s_[:, 3:4], AF.Sqrt, bias=eps_s[:, :], scale=1.0 / 256)
                k.recip(s_[:, 5:6], s_[:, 4:5])
                k.stt(o_[:, :], o_[:, :], s_[:, 5:6], sw[:, :], ALU.mult, ALU.mult)
                for ec in range(2):
                    k.tr(PTf[:, ec, :], o_[:, ec * 128:(ec + 1) * 128], cm[:, C_ID, :])
                k.copy("scalar", yr[:, :, qb * 128:(qb + 1) * 128], PTf[:, 0:2, :])
            for ec in range(2):
                k.dma(TV(ybT.base[2 * hd + ec, :, qsl], ybT.buf), yr[:, ec, :])
    k.barrier()
    st.close()


def pack_A(inp, l, core, consts, mods, vfirst=None):
    b, h = core // 2, core % 2
    cm, m01, cosT, sinT = consts
    W = inp["w_in"][l]
    R0 = 0
    cols = []
    f0 = h * 1024
    for base in (0, 2048, 4096):
        cols.append(W[:, base + f0:base + f0 + 1024])
    lo = 6144

    def pad(a, n=128):
        return np.concatenate([a, np.zeros((a.shape[0], n - a.shape[1]), np.float32)], 1)
    for i in range(4):
        cols.append(pad(W[:, lo + 96 * i:lo + 96 * (i + 1)]))
    cols.append(W[:, lo + 384:lo + 640])
    q0 = 6784
    for base in (q0, q0 + 2048, q0 + 4096):
        cols.append(W[:, base + f0:base + f0 + 1024])
    if l == 1:
        cols.append(pad(inp["vres_down"][0]))
    Wc = np.ascontiguousarray(np.concatenate(cols, 1))
    vec = np.zeros((128, NV), np.float32)
    mp, mn = inp["shift_mu_prev"][l], inp["shift_mu_next"][l]

    def rowvec(v):
        out = np.zeros((128, 30), np.float32)
        for blk in range(3):
            out[:, blk * 8:(blk + 1) * 8] = v[blk * 2048 + f0:blk * 2048 + f0 + 1024].reshape(8, 128).T
        for i in range(4):
            out[0:96, 24 + i] = v[lo + 96 * i:lo + 96 * (i + 1)]
        out[:, 28:30] = v[lo + 384:lo + 640].reshape(2, 128).T
        return out
    vec[:, V_MP:V_MP + 30] = rowvec(mp)
    vec[:, V_MN:V_MN + 30] = rowvec(mn)
    my = lambda v: np.asarray(v, np.float32).reshape(-1)[f0:f0 + 1024].reshape(8, 128).T
    plist = [inp["k_k"][l], inp["k_a"][l], inp["r_k"][l], inp["ln_x_w"][l], inp["ln_x_b"][l],
             inp["decay_w0"][l][0], inp["decay_w0"][l][1], inp["iclr_a0"][l][0], inp["iclr_a0"][l][1],
             inp["vres_v0"][0] if l == 1 else np.zeros(2048, np.float32)]
    for i, v in enumerate(plist):
        vec[:, V_PF + 8 * i:V_PF + 8 * i + 8] = my(v)

    def lora_w(w2):
        out = np.zeros((128, 2, 8, 128), np.float32)
        for d in range(2):
            out[0:96, d] = w2[d][:, f0:f0 + 1024].reshape(96, 8, 128)
        return out
    lwd = lora_w(inp["decay_up"][l])
    lwi = lora_w(inp["iclr_up"][l])
    lwg = np.ascontiguousarray(inp["gate_up"][l][:, f0:f0 + 1024].reshape(2, 128, 8, 128).transpose(1, 0, 2, 3))
    lwv = np.zeros((128, 8, 128), np.float32)
    if l == 1:
        lwv[0:64] = inp["vres_up"][0][:, f0:f0 + 1024].reshape(64, 8, 128)
    lamv = np.ascontiguousarray(np.broadcast_to(np.stack([inp[n][l] for n in ("lambda_q1", "lambda_k1", "lambda_q2", "lambda_k2")])[None], (128, 4, 128)))
    subw = np.ascontiguousarray(np.broadcast_to(inp["subln_w"][l][None], (128, 256)))
    d = {"xT": mods["xT"][b], "modT": mods["mod"][b][l], "Wc": Wc, "vec": vec, "lwd": lwd, "lwi": lwi, "lwg": lwg, "lwv": lwv,
         "lamv": lamv, "subw": subw, "cosT": cosT, "sinT": sinT, "cmat": cm, "m01": m01}
    if l == 1:
        d["vf_in"] = vfirst[core]
    return d


def build_A(l):
    nc = bass.Bass("TRN2", target_bir_lowering=False)
    NRW = 54 + (1 if l == 1 else 0)
    with ExitStack() as st:
        k = K(nc, st)
        IN = lambda n, s: k.dram(n, s, kind="ExternalInput")
        xT = IN("xT", [KC, 128, T])
        modT = IN("modT", [128, 96])
        Wc = IN("Wc", [D, NRW * 128])
        vec = IN("vec", [128, NV])
        lwd, lwi, lwg = [IN(n, [128, 2, 8, 128]) for n in ("lwd", "lwi", "lwg")]
        lwv = IN("lwv", [128, 8, 128])
        lamv = IN("lamv", [128, 4, 128])
        subw = IN("subw", [128, 256])
        cosT, sinT = IN("cosT", [128, T]), IN("sinT", [128, T])
        cmat = IN("cmat", [128, NCM, 128])
        m01 = IN("m01", [128, TG])
        vf_in = IN("vf_in", [8, 128, T]) if l == 1 else None
        yaT = k.dram("yaT", [8, 128, T], kind="ExternalOutput")
        ybT = k.dram("ybT", [8, 128, T], kind="ExternalOutput")
        vf_out = k.dram("vf_out", [8, 128, T], kind="ExternalOutput") if l == 0 else None
        Zs = k.dram("Zs", [NRW, 128, T + 2], kind="Internal")
        ZsB = [k.p.buf() for _ in range(NRW)]
        xsrc = lambda i: [(slice(0, KC), TV(xT.base[:, :, i * 128:(i + 1) * 128].rearrange("c p t -> p c t"), xT.buf))]
        emit_phaseA(k, l, xsrc, modT, Wc, vec, lwd, lwi, lwg, lwv, lamv, subw, cosT, sinT, cmat, m01, vf_in, yaT, ybT, vf_out, Zs, ZsB)
        k.p.finish()
        k.p.emit()
    return nc


_CACHE = {}


def _prog(name, fn):
    if name not in _CACHE:
        _CACHE[name] = fn()
    return _CACHE[name]


def _run(nc, ims):
    return run_bass_kernel_spmd(nc, ims, core_ids=list(range(8))).results


def pack_B(inp, l, core, xT_full, ya, yb, mod):
    b, h = core // 2, core % 2
    tok = slice(h * NT, (h + 1) * NT)
    W = inp["w_in"][l]
    return {"xT": np.ascontiguousarray(xT_full[b][:, :, tok]),
            "yaT": np.ascontiguousarray(ya[b][:, :, tok]), "ybT": np.ascontiguousarray(yb[b][:, :, tok]),
            "wga": np.ascontiguousarray(W[:, 12928:14976]), "wgb": np.ascontiguousarray(W[:, 14976:17024]),
            "modT": mod[b][l],
            "lnv": np.ascontiguousarray(np.stack([_fm(inp[n][l], 16) for n in ("ln1_g", "ln1_b", "ln2_g", "ln2_b")], 1)),
            "proj_a": inp["proj_a"][l], "proj_b": inp["proj_b"][l], "w_out": inp["w_out"][l],
            "wg": inp["ffn_w_gate"][l], "wu": inp["ffn_w_up"][l], "wd": inp["ffn_w_down"][l]}


def kernel(**inputs):
    inp = {k_: np.asarray(v, np.float32) for k_, v in inputs.items()}
    consts = host_consts()
    ncM = _prog("M", build_M)
    ada_bT = np.stack([_fm(inp["ada_b"][ll], 96) for ll in range(2)])
    resM = _run(ncM, [{"cT": _fm(inp["c"][core // 2], 16), "ada_w": inp["ada_w"], "ada_bT": ada_bT} for core in range(8)])
    mod = [[np.ascontiguousarray(resM[2 * b]["mod"][ll]) for ll in range(2)] for b in range(4)]
    x_cur = inp["x"]
    vfirst = None
    ncB = _prog("B", build_B)
    for l in range(2):
        xT_full = [np.ascontiguousarray(x_cur[b].T.reshape(KC, 128, T)) for b in range(4)]
        ncA = _prog("A%d" % l, lambda: build_A(l))
        mods = {"xT": xT_full, "mod": mod}
        resA = _run(ncA, [pack_A(inp, l, core, consts, mods, vfirst) for core in range(8)])
        ya = [np.concatenate([resA[2 * b]["yaT"], resA[2 * b + 1]["yaT"]], 0) for b in range(4)]
        yb = [np.concatenate([resA[2 * b]["ybT"], resA[2 * b + 1]["ybT"]], 0) for b in range(4)]
        if l == 0:
            vfirst = [resA[c]["vf_out"] for c in range(8)]
        resB = _run(ncB, [pack_B(inp, l, core, xT_full, ya, yb, mod) for core in range(8)])
        x_next = np.empty_like(x_cur)
        for core in range(8):
            b, h = core // 2, core % 2
            x_next[b, h * NT:(h + 1) * NT, :] = resB[core]["x_out"].reshape(D, NT).T
        x_cur = x_next
    return x_cur


A_NAMES = ("Wc", "vec", "lwd", "lwi", "lwg", "lwv", "lamv", "subw")
B_NAMES = ("lnv", "proj_a", "proj_b", "w_out", "wga", "wgb", "wg", "wu", "wd")


def build_fused():
    nc = bass.Bass("TRN2", target_bir_lowering=False)
    with ExitStack() as st:
        k = K(nc, st)
        IN = lambda n, s: k.dram(n, s, kind="ExternalInput")
        cT = IN("cT", [128, KC])
        ada_w = IN("ada_w", [2, D, 6 * D])
        ada_bT = IN("ada_bT", [2, 128, 96])
        xT = IN("xT", [KC, 128, T])
        xTo = IN("xTo", [KC, 128, NT])
        sel = IN("sel", [128, 2])
        cosT, sinT = IN("cosT", [128, T]), IN("sinT", [128, T])
        cmat = IN("cmat", [128, NCM, 128])
        m01 = IN("m01", [128, TG])
        A_in, B_in = [], []
        for l in range(2):
            NRW = 54 + l
            A_in.append(dict(Wc=IN("Wc%d" % l, [D, NRW * 128]), vec=IN("vec%d" % l, [128, NV]),
                             lwd=IN("lwd%d" % l, [128, 2, 8, 128]), lwi=IN("lwi%d" % l, [128, 2, 8, 128]),
                             lwg=IN("lwg%d" % l, [128, 2, 8, 128]), lwv=IN("lwv%d" % l, [128, 8, 128]),
                             lamv=IN("lamv%d" % l, [128, 4, 128]), subw=IN("subw%d" % l, [128, 256])))
            B_in.append(dict(lnv=IN("lnv%d" % l, [128, 4, KC]), proj_a=IN("proj_a%d" % l, [D, D]), proj_b=IN("proj_b%d" % l, [D, D]),
                             w_out=IN("w_out%d" % l, [D, D]), wga=IN("wga%d" % l, [D, D]), wgb=IN("wgb%d" % l, [D, D]),
                             wg=IN("wg%d" % l, [D, FH]), wu=IN("wu%d" % l, [D, FH]), wd=IN("wd%d" % l, [FH, D])))
        x_out = k.dram("x_out", [KC, 128, NT], kind="ExternalOutput")
        ITL = lambda n, s: TL(nc.dram_tensor(n, list(s), F32).ap(), k.p.buf())
        modD = ITL("modD", [2, 128, 96])
        Zs = ITL("Zs", [55, 128, T + 2])
        ZsB = [k.p.buf() for _ in range(55)]
        vf = ITL("vf", [8, 128, T])
        ya = [ITL("ya%d" % l, [8 * 128, T]) for l in range(2)]
        yb = [ITL("yb%d" % l, [8 * 128, T]) for l in range(2)]
        GA = [ITL("GA%d" % l, [8, 2 * 128, T]) for l in range(2)]
        GB = [ITL("GB%d" % l, [8, 2 * 128, T]) for l in range(2)]
        x1 = ITL("x1", [KC * 128, NT])
        GX = ITL("GX", [8, 2 * 256, NT])
        v3 = lambda tl, n: TL(tl.base.rearrange("(c p) t -> c p t", p=128), tl.buf)
        v4 = lambda tl: TL(tl.base.rearrange("c (r p) t -> r c p t", r=2), tl.buf)

        emit_mod(k, cT, ada_w, ada_bT, modD)
        for l in range(2):
            a, b = A_in[l], B_in[l]
            modT = TL(modD.base[l], modD.buf)
            if l == 0:
                xsrc = lambda i: [(slice(0, KC), TV(xT.base[:, :, i * 128:(i + 1) * 128].rearrange("c p t -> p c t"), xT.buf))]
                x_own = xTo
            else:
                gx5 = GX.base.rearrange("j (r cc p) t -> r j cc p t", r=2, cc=2)
                xsrc = lambda i: [(slice(cc, KC, 2), TV(gx5[i // 16, :, cc, :, (i % 16) * 128:(i % 16 + 1) * 128].rearrange("j p t -> p j t"), GX.buf))
                                  for cc in range(2)]
                x_own = v3(x1, KC)
            emit_phaseA(k, l, xsrc, modT, a["Wc"], a["vec"], a["lwd"], a["lwi"], a["lwg"], a["lwv"], a["lamv"], a["subw"],
                        cosT, sinT, cmat, m01, vf if l == 1 else None, v3(ya[l], 8), v3(yb[l], 8), vf if l == 0 else None, Zs, ZsB)
            k.allgather_pieces(GA[l], ya[l], 128)
            k.allgather_pieces(GB[l], yb[l], 128)
            xo = v3(x1, KC) if l == 0 else x_out
            emit_phase3(k, l, x_own, v4(GA[l]), v4(GB[l]), sel, b["wga"], b["wgb"], modT, b["lnv"], b["proj_a"], b["proj_b"], b["w_out"],
                        b["wg"], b["wu"], b["wd"], xo, None)
            if l == 0:
                k.allgather_pieces(GX, x1, 256)
        k.p.finish()
        k.p.emit()
    return nc


def kernel(**inputs):
    inp = {k_: np.asarray(v, np.float32) for k_, v in inputs.items()}
    consts = host_consts()
    cm, m01, cosT, sinT = consts
    nc = _prog("F", build_fused)
    ada_bT = np.stack([_fm(inp["ada_b"][ll], 96) for ll in range(2)])
    xT_full = [np.ascontiguousarray(inp["x"][b].T.reshape(KC, 128, T)) for b in range(4)]
    ims = []
    for core in range(8):
        b, h = core // 2, core % 2
        sel = np.zeros((128, 2), np.float32)
        sel[:, h] = 1.0
        d = {"cT": _fm(inp["c"][b], 16), "ada_w": inp["ada_w"], "ada_bT": ada_bT, "xT": xT_full[b],
             "xTo": np.ascontiguousarray(xT_full[b][:, :, h * NT:(h + 1) * NT]), "sel": sel,
             "cosT": cosT, "sinT": sinT, "cmat": cm, "m01": m01}
        for l in range(2):
            pa = pack_A(inp, l, core, consts, {"xT": xT_full, "mod": [[None, None]] * 4}, [None] * 8)
            for n in A_NAMES:
                d["%s%d" % (n, l)] = pa[n]
            W = inp["w_in"][l]
            d["lnv%d" % l] = np.ascontiguousarray(np.stack([_fm(inp[n][l], 16) for n in ("ln1_g", "ln1_b", "ln2_g", "ln2_b")], 1))
            d["proj_a%d" % l] = inp["proj_a"][l]
            d["proj_b%d" % l] = inp["proj_b"][l]
            d["w_out%d" % l] = inp["w_out"][l]
            d["wga%d" % l] = np.ascontiguousarray(W[:, 12928:14976])
            d["wgb%d" % l] = np.ascontiguousarray(W[:, 14976:17024])
            d["wg%d" % l] = inp["ffn_w_gate"][l]
            d["wu%d" % l] = inp["ffn_w_up"][l]
            d["wd%d" % l] = inp["ffn_w_down"][l]
        ims.append(d)
    res = _run(nc, ims)
    out = np.empty((4, T, D), np.float32)
    for core in range(8):
        b, h = core // 2, core % 2
        out[b, h * NT:(h + 1) * NT, :] = res[core]["x_out"].reshape(D, NT).T
    return out
```
